# Optimizing a Trainium2 kernel written in Bass

```python
import math
import jax, jax.numpy as jnp
from jax import lax
import numpy as np

D_MODEL = 1024
BATCH = 4
SEQ = 4096
DEPTH = 1
DEC_BATCH = 128
DEC_SEQ = 4
PAST_LEN = 8192
PAGE_SIZE = 128

N_META = 16
D_MIX = D_MODEL
HEAD_DIM = 64
ATT_WIDTH = D_MIX // 2
N_HEADS = ATT_WIDTH // HEAD_DIM
N_KV_HEADS = 2
Q_PER_KV = N_HEADS // N_KV_HEADS
WINDOW = 128
BLOCK = 128
ROPE_DIM = HEAD_DIM // 4
ROPE_THETA = 500000.0
SSM_WIDTH = D_MIX - ATT_WIDTH
SSM_GROUP = 16
SSM_GROUPS = SSM_WIDTH // SSM_GROUP
SSM_STATE = 64
KV_WIDTH = N_KV_HEADS * HEAD_DIM
D_IN = ATT_WIDTH + 2 * KV_WIDTH + SSM_WIDTH
D_FF = 4 * D_MODEL
EPS = 1e-6
NEG = -1e30

kernel_name = "hymba_s5_swa_sink_decode_step"


def rmsnorm(x, g):
    xf = x.astype(jnp.float32)
    y = xf * lax.rsqrt(jnp.mean(xf * xf, axis=-1, keepdims=True) + EPS)
    return (y * g.astype(jnp.float32)).astype(x.dtype)


def partial_rope(x, pos):
    half = ROPE_DIM // 2
    inv = ROPE_THETA ** (-jnp.arange(half, dtype=jnp.float32) / half)
    ang = pos.astype(jnp.float32)[:, None] * inv[None, :]
    cos = jnp.cos(ang)[:, None, :]
    sin = jnp.sin(ang)[:, None, :]
    xr = x[..., :ROPE_DIM].astype(jnp.float32)
    x1, x2 = xr[..., :half], xr[..., half:]
    rot = jnp.concatenate([x1 * cos - x2 * sin, x2 * cos + x1 * sin], axis=-1).astype(x.dtype)
    return jnp.concatenate([rot, x[..., ROPE_DIM:]], axis=-1)


def in_project(h, g_pre, w_in):
    hn = rmsnorm(h, g_pre)
    proj = jnp.einsum('bsd,de->bse', hn, w_in)
    B, S = h.shape[:2]
    q = proj[..., :ATT_WIDTH].reshape(B, S, N_HEADS, HEAD_DIM)
    k = proj[..., ATT_WIDTH:ATT_WIDTH + KV_WIDTH].reshape(B, S, N_KV_HEADS, HEAD_DIM)
    v = proj[..., ATT_WIDTH + KV_WIDTH:ATT_WIDTH + 2 * KV_WIDTH].reshape(B, S, N_KV_HEADS, HEAD_DIM)
    u = proj[..., ATT_WIDTH + 2 * KV_WIDTH:]
    return q, k, v, u


def sink_softmax(logits, mask, sink):
    logits = jnp.where(mask, logits, NEG)
    m = jnp.maximum(jnp.max(logits, axis=-1, keepdims=True), sink)
    e = jnp.exp(logits - m)
    return e / (jnp.sum(e, axis=-1, keepdims=True) + jnp.exp(sink - m))


def swa_prompt(q, k, v, sinks):
    B, L = q.shape[:2]
    front = (BLOCK - N_META % BLOCK) % BLOCK
    back = (-(front + L)) % BLOCK
    pad = ((0, 0), (front, back), (0, 0), (0, 0))
    qp, kp, vp = jnp.pad(q, pad), jnp.pad(k, pad), jnp.pad(v, pad)
    nb = qp.shape[1] // BLOCK
    qb = qp.reshape(B, nb, BLOCK, N_KV_HEADS, Q_PER_KV, HEAD_DIM)
    kb = kp.reshape(B, nb, BLOCK, N_KV_HEADS, HEAD_DIM)
    vb = vp.reshape(B, nb, BLOCK, N_KV_HEADS, HEAD_DIM)
    zblk = jnp.zeros_like(kb[:, :1])
    kk = jnp.concatenate([jnp.concatenate([zblk, kb[:, :-1]], axis=1), kb], axis=2)
    vv = jnp.concatenate([jnp.concatenate([zblk, vb[:, :-1]], axis=1), vb], axis=2)
    pq = (jnp.arange(nb * BLOCK, dtype=jnp.int32) - front).reshape(nb, BLOCK)
    pk = jnp.concatenate([pq - BLOCK, pq], axis=1)
    diff = pq[:, :, None] - pk[:, None, :]
    mask = (diff >= 0) & (diff < WINDOW) & (pk[:, None, :] >= 0)
    logits = jnp.einsum('bnqkgd,bnskd->bnkgqs', qb, kk).astype(jnp.float32) * (HEAD_DIM ** -0.5)
    sink = sinks.astype(jnp.float32).reshape(1, 1, N_KV_HEADS, Q_PER_KV, 1, 1)
    p = sink_softmax(logits, mask[None, :, None, None], sink)
    o = jnp.einsum('bnkgqs,bnskd->bnqkgd', p.astype(v.dtype), vv)
    o = o.reshape(B, nb * BLOCK, ATT_WIDTH)[:, front:front + L]
    return o


def swa_sample(q, k, v, cache_k, cache_v, pos_q, sinks):
    w = cache_k.shape[1]
    keys = jnp.concatenate([cache_k.astype(k.dtype), k], axis=1)
    vals = jnp.concatenate([cache_v.astype(v.dtype), v], axis=1)
    pos_k = jnp.concatenate([pos_q[0] - w + jnp.arange(w, dtype=jnp.int32), pos_q])
    diff = pos_q[:, None] - pos_k[None, :]
    mask = (diff >= 0) & (diff < WINDOW)
    B, T = q.shape[:2]
    qg = q.reshape(B, T, N_KV_HEADS, Q_PER_KV, HEAD_DIM)
    logits = jnp.einsum('btkgd,bskd->bkgts', qg, keys).astype(jnp.float32) * (HEAD_DIM ** -0.5)
    sink = sinks.astype(jnp.float32).reshape(1, N_KV_HEADS, Q_PER_KV, 1, 1)
    p = sink_softmax(logits, mask[None, None, None], sink)
    o = jnp.einsum('bkgts,bskd->btkgd', p.astype(v.dtype), vals).reshape(B, T, ATT_WIDTH)
    return o, keys[:, -WINDOW:], vals[:, -WINDOW:]


def zoh(a_re, a_im, log_dt, b_re, b_im):
    dt = jnp.exp(log_dt.astype(jnp.float32))[:, None]
    ar, ai = a_re.astype(jnp.float32), a_im.astype(jnp.float32)
    mag = jnp.exp(ar * dt)
    abar_re, abar_im = mag * jnp.cos(ai * dt), mag * jnp.sin(ai * dt)
    nr, ni = abar_re - 1.0, abar_im
    den = ar * ar + ai * ai
    coef_re = (nr * ar + ni * ai) / den
    coef_im = (ni * ar - nr * ai) / den
    br, bi = b_re.astype(jnp.float32), b_im.astype(jnp.float32)
    bbar_re = coef_re[..., None] * br - coef_im[..., None] * bi
    bbar_im = coef_re[..., None] * bi + coef_im[..., None] * br
    return abar_re, abar_im, bbar_re, bbar_im


def cplx_combine(e1, e2):
    ar1, ai1, br1, bi1 = e1
    ar2, ai2, br2, bi2 = e2
    return (ar2 * ar1 - ai2 * ai1,
            ar2 * ai1 + ai2 * ar1,
            ar2 * br1 - ai2 * bi1 + br2,
            ar2 * bi1 + ai2 * br1 + bi2)


def ssm_mixer(u, h0_re, h0_im, a_re, a_im, log_dt, b_re, b_im, c_re, c_im, d, w_glu):
    B, S = u.shape[:2]
    abar_re, abar_im, bbar_re, bbar_im = zoh(a_re, a_im, log_dt, b_re, b_im)
    ug = u.astype(jnp.float32).reshape(B, S, SSM_GROUPS, SSM_GROUP)
    bu_re = jnp.einsum('bsgh,gph->bsgp', ug, bbar_re)
    bu_im = jnp.einsum('bsgh,gph->bsgp', ug, bbar_im)
    if h0_re is not None:
        hr0, hi0 = h0_re.astype(jnp.float32), h0_im.astype(jnp.float32)
        bu_re = bu_re.at[:, 0].add(abar_re * hr0 - abar_im * hi0)
        bu_im = bu_im.at[:, 0].add(abar_re * hi0 + abar_im * hr0)
    ar = jnp.broadcast_to(abar_re, bu_re.shape)
    ai = jnp.broadcast_to(abar_im, bu_re.shape)
    _, _, hr, hi = lax.associative_scan(cplx_combine, (ar, ai, bu_re, bu_im), axis=1)
    y = (jnp.einsum('bsgp,ghp->bsgh', hr, c_re.astype(jnp.float32))
         - jnp.einsum('bsgp,ghp->bsgh', hi, c_im.astype(jnp.float32))
         + d.astype(jnp.float32) * ug).reshape(B, S, SSM_WIDTH)
    z = jax.nn.gelu(y)
    out = z * jax.nn.sigmoid(jnp.einsum('bsc,ce->bse', z, w_glu.astype(jnp.float32)))
    return out.astype(u.dtype), hr[:, -1], hi[:, -1]


def merge_out(att_o, ssm_o, g_att, g_ssm, w_out, g_post):
    cat = jnp.concatenate([rmsnorm(att_o, g_att), rmsnorm(ssm_o, g_ssm)], axis=-1)
    return rmsnorm(jnp.einsum('bse,ed->bsd', cat, w_out), g_post)


def sqrelu_mlp(h, g_pre, w_up, w_down, g_post):
    hn = rmsnorm(h, g_pre)
    a = jnp.square(jax.nn.relu(jnp.einsum('bsd,df->bsf', hn, w_up)))
    return rmsnorm(jnp.einsum('bsf,fd->bsd', a, w_down), g_post)


def setup_inputs(seed: int = 0) -> dict:
    key = jax.random.key(seed)
    ks = jax.random.split(key, 32)
    f32 = jnp.float32

    def nrm(k, shape, scale):
        return jax.random.normal(k, shape, f32) * scale

    def gain(k, shape):
        return 1.0 + 0.05 * jax.random.normal(k, shape, f32)

    w_cache = min(WINDOW, PAST_LEN)
    n_idx = jnp.arange(SSM_STATE, dtype=f32)
    return {
        'x_prompt': nrm(ks[0], (BATCH, SEQ, D_MODEL), 1.0),
        'x_sample': nrm(ks[1], (DEC_BATCH, DEC_SEQ, D_MODEL), 1.0),
        'cache_k_win': nrm(ks[2], (DEPTH, DEC_BATCH, w_cache, N_KV_HEADS, HEAD_DIM), 1.0),
        'cache_v_win': nrm(ks[3], (DEPTH, DEC_BATCH, w_cache, N_KV_HEADS, HEAD_DIM), 1.0),
        'state_ssm_re': nrm(ks[4], (DEPTH, DEC_BATCH, SSM_GROUPS, SSM_STATE), 0.5),
        'state_ssm_im': nrm(ks[5], (DEPTH, DEC_BATCH, SSM_GROUPS, SSM_STATE), 0.5),
        'meta_tokens': nrm(ks[6], (N_META, D_MODEL), 1.0),
        'norm_mix_pre': gain(ks[7], (DEPTH, D_MODEL)),
        'w_in': nrm(ks[8], (DEPTH, D_MODEL, D_IN), D_MODEL ** -0.5),
        'attn_sinks': nrm(ks[9], (DEPTH, N_HEADS), 0.5),
        'ssm_a_re': -0.5 + 0.01 * jax.random.normal(ks[10], (DEPTH, SSM_GROUPS, SSM_STATE), f32),
        'ssm_a_im': math.pi * n_idx + 0.01 * jax.random.normal(ks[11], (DEPTH, SSM_GROUPS, SSM_STATE), f32),
        'ssm_log_dt': jax.random.uniform(ks[12], (DEPTH, SSM_GROUPS), f32, math.log(0.001), math.log(0.1)),
        'ssm_b_re': nrm(ks[13], (DEPTH, SSM_GROUPS, SSM_STATE, SSM_GROUP), (2 * SSM_GROUP) ** -0.5),
        'ssm_b_im': nrm(ks[14], (DEPTH, SSM_GROUPS, SSM_STATE, SSM_GROUP), (2 * SSM_GROUP) ** -0.5),
        'ssm_c_re': nrm(ks[15], (DEPTH, SSM_GROUPS, SSM_GROUP, SSM_STATE), (2 * SSM_STATE) ** -0.5),
        'ssm_c_im': nrm(ks[16], (DEPTH, SSM_GROUPS, SSM_GROUP, SSM_STATE), (2 * SSM_STATE) ** -0.5),
        'ssm_d': nrm(ks[17], (DEPTH, SSM_GROUPS, SSM_GROUP), 1.0),
        'w_glu': nrm(ks[18], (DEPTH, SSM_WIDTH, SSM_WIDTH), SSM_WIDTH ** -0.5),
        'norm_att_out': gain(ks[19], (DEPTH, ATT_WIDTH)),
        'norm_ssm_out': gain(ks[20], (DEPTH, SSM_WIDTH)),
        'w_out': nrm(ks[21], (DEPTH, D_MIX, D_MODEL), D_MIX ** -0.5),
        'norm_mix_post': gain(ks[22], (DEPTH, D_MODEL)),
        'norm_mlp_pre': gain(ks[23], (DEPTH, D_MODEL)),
        'w_up': nrm(ks[24], (DEPTH, D_MODEL, D_FF), D_MODEL ** -0.5),
        'w_down': nrm(ks[25], (DEPTH, D_FF, D_MODEL), D_FF ** -0.5),
        'norm_mlp_post': gain(ks[26], (DEPTH, D_MODEL)),
    }


def reference(x_prompt, x_sample, cache_k_win, cache_v_win, state_ssm_re, state_ssm_im,
              meta_tokens, norm_mix_pre, w_in, attn_sinks, ssm_a_re, ssm_a_im, ssm_log_dt,
              ssm_b_re, ssm_b_im, ssm_c_re, ssm_c_im, ssm_d, w_glu, norm_att_out, norm_ssm_out,
              w_out, norm_mix_post, norm_mlp_pre, w_up, w_down, norm_mlp_post):
    B = x_prompt.shape[0]
    meta = jnp.broadcast_to(meta_tokens[None].astype(x_prompt.dtype), (B, N_META, D_MODEL))
    hp = jnp.concatenate([meta, x_prompt], axis=1)
    hs = x_sample
    pos_p = jnp.arange(hp.shape[1], dtype=jnp.int32)
    pos_s = PAST_LEN + jnp.arange(hs.shape[1], dtype=jnp.int32)

    kp_l, vp_l, srp_l, sip_l = [], [], [], []
    ks_l, vs_l, srs_l, sis_l = [], [], [], []
    for l in range(DEPTH):
        ssm_p = (ssm_a_re[l], ssm_a_im[l], ssm_log_dt[l], ssm_b_re[l], ssm_b_im[l],
                 ssm_c_re[l], ssm_c_im[l], ssm_d[l], w_glu[l])
        q, k, v, u = in_project(hp, norm_mix_pre[l], w_in[l])
        q, k = partial_rope(q, pos_p), partial_rope(k, pos_p)
        att_o = swa_prompt(q, k, v, attn_sinks[l])
        ssm_o, hr, hi = ssm_mixer(u, None, None, *ssm_p)
        hp = hp + merge_out(att_o, ssm_o, norm_att_out[l], norm_ssm_out[l], w_out[l], norm_mix_post[l])
        hp = hp + sqrelu_mlp(hp, norm_mlp_pre[l], w_up[l], w_down[l], norm_mlp_post[l])
        kp_l.append(k[:, -WINDOW:])
        vp_l.append(v[:, -WINDOW:])
        srp_l.append(hr.astype(x_prompt.dtype))
        sip_l.append(hi.astype(x_prompt.dtype))
        q, k, v, u = in_project(hs, norm_mix_pre[l], w_in[l])
        q, k = partial_rope(q, pos_s), partial_rope(k, pos_s)
        att_o, nk, nv = swa_sample(q, k, v, cache_k_win[l], cache_v_win[l], pos_s, attn_sinks[l])
        ssm_o, hr, hi = ssm_mixer(u, state_ssm_re[l], state_ssm_im[l], *ssm_p)
        hs = hs + merge_out(att_o, ssm_o, norm_att_out[l], norm_ssm_out[l], w_out[l], norm_mix_post[l])
        hs = hs + sqrelu_mlp(hs, norm_mlp_pre[l], w_up[l], w_down[l], norm_mlp_post[l])
        ks_l.append(nk)
        vs_l.append(nv)
        srs_l.append(hr.astype(state_ssm_re.dtype))
        sis_l.append(hi.astype(state_ssm_im.dtype))

    y_prompt = hp[:, N_META:]
    y_sample = hs
    return (y_prompt, y_sample,
            jnp.stack(kp_l), jnp.stack(vp_l), jnp.stack(srp_l), jnp.stack(sip_l),
            jnp.stack(ks_l), jnp.stack(vs_l), jnp.stack(srs_l), jnp.stack(sis_l))
```

```python
import math
from contextlib import ExitStack
import numpy as np
import concourse.bass as bass
import concourse.mybir as mybir
from concourse.bass_utils import run_bass_kernel_spmd

F32 = mybir.dt.float32
BF16 = mybir.dt.bfloat16
I32 = mybir.dt.int32
ALU = mybir.AluOpType
AF = mybir.ActivationFunctionType

NPRE = 256
NMAIN = 272
NT = NPRE + NMAIN
NTILE_P = 16
NTILE_M = 17
NCOL = 2240
TWO_PI_S = 6.2831
EPS = 1e-6


class Res:
    __slots__ = ("w", "r")

    def __init__(self):
        self.w = None
        self.r = []


class K:
    NDMA = 6

    def __init__(self, nc, stack):
        self.nc = nc
        self.eng = {"pe": nc.tensor, "dve": nc.vector, "act": nc.scalar,
                    "pool": nc.gpsimd, "sp": nc.sync}
        self.sems = {}
        self.cnt = {}
        self.seen = {e: {} for e in self.eng}
        for e in self.eng:
            self.sems[e] = stack.enter_context(nc.semaphore("s_" + e))
            self.cnt[e] = 0
        self.dq = {}
        for q, nq in (("sp", 24), ("pool", 12), ("act", 2)):
            lst = []
            for i in range(nq):
                key = "d_%s%d" % (q, i)
                self.sems[key] = stack.enter_context(nc.semaphore(key))
                self.cnt[key] = 0
                lst.append(key)
            self.dq[q] = [lst, 0]
        self.n_inst = 0

    def _wait(self, e, dep):
        if dep is None:
            return
        key, val = dep
        if key == "pe" and e == "pe" and (self.pe_chain or val > self.cnt["pe"]):
            return
        if self.seen[e].get(key, 0) >= val:
            return
        self.eng[e].wait_ge(self.sems[key], val)
        self.seen[e][key] = val

    def _deps(self, e, reads, writes):
        for r in reads:
            self._wait(e, r.w)
        for w in writes:
            self._wait(e, w.w)
            for d in w.r:
                self._wait(e, d)

    def _commit(self, tok, reads, writes):
        for r in reads:
            r.r.append(tok)
            if len(r.r) > 24:
                best = {}
                for kk, v in r.r:
                    if best.get(kk, 0) < v:
                        best[kk] = v
                r.r = list(best.items())
        for w in writes:
            w.w = tok
            w.r = []

    pe_chain = False

    def op(self, e, fn, reads=(), writes=(), sig=True, chain=False):
        self.pe_chain = chain and e == "pe"
        self._deps(e, reads, writes)
        self.pe_chain = False
        ins = fn(self.eng[e])
        if sig:
            self.cnt[e] += 1
            ins.then_inc(self.sems[e], 1)
            tok = (e, self.cnt[e])
        else:
            assert e == "pe"
            tok = (e, self.cnt[e] + 1)
        self._commit(tok, reads, writes)
        self.n_inst += 1
        return ins

    def dma(self, q, out, in_, reads=(), writes=(), **kw):
        self._deps(q, reads, writes)
        lst, i = self.dq[q]
        key = lst[i % len(lst)]
        self.dq[q][1] = i + 1
        if self.cnt[key] > 0:
            self._wait(q, (key, self.cnt[key]))
        ins = self.eng[q].dma_start(out=out, in_=in_, **kw)
        self.cnt[key] += 16
        ins.then_inc(self.sems[key], 16)
        self._commit((key, self.cnt[key]), reads, writes)
        self.n_inst += 1
        return ins

    def wait_all(self, e, ress):
        for r in ress:
            self._wait(e, r.w)
            for d in r.r:
                self._wait(e, d)

    def barrier(self):
        for e in self.eng:
            for key in self.sems:
                if self.cnt[key] > 0 and not (key == "pe" and e == "pe"):
                    self._wait(e, (key, self.cnt[key]))


import os
_SITES = set(os.environ.get("KSITES", "ALL").split(","))


def SG(site, cond):
    return bool(cond) if ("ALL" in _SITES or site in _SITES) else True


def CH(site, cond):
    return bool(cond) if ("ALL" in _SITES or site in _SITES) else False


def run_staged(gens, nstages, oldest_first=False, order=None):
    gens = list(gens)
    n = len(gens)
    for it in range(n + nstages - 1):
        for s in (order if order is not None else (range(nstages - 1, -1, -1) if oldest_first else range(nstages))):
            i = it - s
            if 0 <= i < n:
                try:
                    next(gens[i])
                except StopIteration:
                    pass
    for g_ in gens:
        for _ in g_:
            pass


def run_pipelined(gens, depth=1):
    gens = list(gens)
    n = len(gens)
    alive = [True] * n

    def step(i):
        if alive[i]:
            try:
                next(gens[i])
            except StopIteration:
                alive[i] = False

    for i in range(min(depth, n)):
        step(i)
    for i in range(n):
        if i + depth < n:
            step(i + depth)
        while alive[i]:
            step(i)


def bc(ap, axis, shape):
    return ap.unsqueeze(axis).to_broadcast(list(shape))


def build_program(debug=False):
    nc = bass.Bass("TRN2", target_bir_lowering=False)

    def DI(name, shape, dt=F32):
        return nc.dram_tensor(name, list(shape), dt, kind="ExternalInput").ap()

    def DO(name, shape, dt=F32):
        return nc.dram_tensor(name, list(shape), dt, kind="ExternalOutput").ap()

    def DS(name, shape, dt=F32):
        return nc.dram_tensor(name, list(shape), dt, kind="Internal").ap()

    xm = DI("xm", [2176, 1024]); xp = DI("xp", [2048, 1024]); xs = DI("xs", [64, 1024])
    pos_d = DI("pos", [128, 18]); invf_d = DI("invf", [128, 8])
    mask1_d = DI("mask1", [128, 128]); maskp_d = DI("maskp", [128, 128]); maskc_d = DI("maskc", [128, 128])
    masks_d = DI("masks", [128, 64]); maskM_d = DI("maskM", [128, 128]); iota_d = DI("iota_n", [128, NT])
    kv_d = DI("kvals", [128, 16]); kvr_d = DI("kvals_r", [128, 16])
    w_in_d = DI("w_in", [1024, 1408]); w_out_d = DI("w_out", [1024, 1024])
    w_up_d = DI("w_up", [1024, 4096]); w_down_d = DI("w_down", [4096, 1024]); w_glu_d = DI("w_glu", [512, 512])
    gpre_d = DI("gpre", [128, 8]); gcat_d = DI("gcat", [128, 8]); gmlp_d = DI("gmlp", [128, 8])
    gpost_d = DI("gpost", [128, 1024]); gmpost_d = DI("gmpost", [128, 1024])
    sink_d = DI("sinks", [128, 8]); dvec_d = DI("dvec", [128, 4])
    art_d = DI("art", [128, 32]); ait_d = DI("ait", [128, 32]); ldt_d = DI("ldt", [128, 32])
    btr_d = DI("btr", [128, 32, 16]); bti_d = DI("bti", [128, 32, 16])
    ctr_d = DI("ctr", [128, 32, 16]); cti_d = DI("cti", [128, 32, 16])
    ckd_d = DI("ckd", [16, 128, 256]); ck_d = DI("ck", [16, 128, 128]); cv_d = DI("cv", [16, 128, 128])
    h0s_d = DI("h0s", [128, 32, 16]); h0d_d = DI("h0d", [128, 32, 16])

    y_main = DO("y_main", [2048, 1024]); y_s = DO("y_s", [64, 1024])
    kwin = DO("kwin", [128, 128]); vwin = DO("vwin", [128, 128]); hend_o = DO("hend", [128, 32])
    kc_s = DO("kc_s", [16, 124, 128]); vc_s = DO("vc_s", [16, 124, 128])
    knew = DO("knew", [64, 128]); vnew = DO("vnew", [64, 128]); hs_new = DO("hs_new", [128, 32, 16])

    Ud = DS("Ud", [4, 128, 8, NT], BF16); Usd = DS("Usd", [4, 128, 4, 16], BF16)
    Yd = DS("Yd", [8, 512, NMAIN], BF16); Ysd = DS("Ysd", [4, 512, 16], BF16)

    with ExitStack() as top:
        k = K(nc, top)

        def SB(st, name, shape, dt=F32):
            return st.enter_context(nc.sbuf_tensor("sb_" + name, list(shape), dt)), Res()

        def PS(st, name, shape, dt=F32):
            n = int(np.prod(shape[1:]))
            per_bank = 512 if dt == F32 else 1024
            nb = -(-n // per_bank)
            t = st.enter_context(nc.psum_tensor("ps_" + name, [128, nb * per_bank], dt))
            v = t[:, 0:n]
            if len(shape) > 2:
                names = ["d%d" % i for i in range(len(shape) - 1)]
                pat = "p (%s) -> p %s" % (" ".join(names), " ".join(names))
                v = v.rearrange(pat, **{nm: int(sz) for nm, sz in zip(names, shape[1:])})
            return v, Res()

        identf, r_identf = SB(top, "identf", [128, 128])
        ident, r_ident = SB(top, "ident", [128, 128], BF16)
        k.op("pool", lambda e: e.memset(identf[:, :], 0.0), writes=[r_identf])
        k.op("pool", lambda e: e.affine_select(identf[:, :], identf[:, :], [[1, 128]], ALU.not_equal, 1.0,
                                               base=0, channel_multiplier=-1),
             reads=[r_identf], writes=[r_identf])
        k.op("dve", lambda e: e.tensor_copy(ident[:, :], identf[:, :]), reads=[r_identf], writes=[r_ident])
        onesb, r_onesb = SB(top, "onesb", [128, 128], BF16)
        k.op("pool", lambda e: e.memset(onesb[:, :], 1.0), writes=[r_onesb])

        def ld(st, name, dram, shape, dt=F32, q="sp"):
            t, r = SB(st, name, shape, dt)
            idx = tuple(slice(None) for _ in shape)
            k.dma(q, t[idx], dram, writes=[r])
            return t, r

        epsb, r_epsb = SB(top, "epsb", [128, 1])
        k.op("pool", lambda e: e.memset(epsb[:, :], EPS), writes=[r_epsb])
        hpi, r_hpi = SB(top, "hpi", [128, 1])
        k.op("pool", lambda e: e.memset(hpi[:, :], math.pi / 2), writes=[r_hpi])
        catT, r_catT = SB(top, "catT", [128, 8, NCOL], BF16)

        stAC = ExitStack()
        top.enter_context(stAC)
        Mbf, r_Mbf = SB(stAC, "Mbf", [128, 32, 128], BF16)
        Pbf, r_Pbf = SB(stAC, "Pbf", [128, 32, 128], BF16)
        Pdbf, r_Pdbf = SB(stAC, "Pdbf", [128, 32, 128], BF16)
        Qbf, r_Qbf = SB(stAC, "Qbf", [128, 32, 128], BF16)
        R8, r_R8 = SB(stAC, "R8", [128, 32])
        TH8, r_TH8 = SB(stAC, "TH8", [128, 32])
        E4r, r_E4 = SB(stAC, "E4r", [128, 32])
        E4i, _ = SB(stAC, "E4i", [128, 32])
        UX, _ = SB(stAC, "UX", [128, 4 * 8 * NT], BF16)
        UT = UX[:, :].rearrange("p (c s n) -> p c s n", c=4, s=8); r_UT = Res()
        USJ = UX[:, :].rearrange("p (g n) -> p g n", g=32)
        r_USJl = [Res() for _ in range(8)]
        stAB = ExitStack()
        top.enter_context(stAB)
        qT, r_qT = SB(stAB, "qT", [128, 17, 4, 128], BF16)
        kT, r_kT = SB(stAB, "kT", [128, 18, 2, 128], BF16)
        Vall, r_V = SB(stAB, "Vall", [128, 18, 2, 65], BF16)
        k.op("pool", lambda e: e.memset(Vall[:, :, :, :], 1.0), writes=[r_V])
        k.op("pool", lambda e: e.memset(Vall[64:128, 17, :, 0:64], 0.0), writes=[r_V])

        stA = ExitStack()
        with stA:
            win, r_win = SB(stA, "win", [128, 8, 1408], BF16)
            k.dma("pool", win[:, :, :], w_in_d.rearrange("(c p) n -> p c n", p=128), writes=[r_win])
            gpre, r_gpre = ld(stA, "gpre", gpre_d, [128, 8])
            pos, r_pos = ld(stA, "pos", pos_d, [128, 18])
            invf, r_invf = ld(stA, "invf", invf_d, [128, 8])
            rt, r_rt = SB(stA, "rt", [128, 18, 8])
            rti, r_rti = SB(stA, "rti", [128, 18, 8], I32)
            rfr, r_rfr = SB(stA, "rfr", [128, 18, 8])
            rab, r_rab = SB(stA, "rab", [128, 18, 8])
            rsin, r_rsin = SB(stA, "rsin", [128, 18, 8])
            rcos, r_rcos = SB(stA, "rcos", [128, 18, 8])
            k.op("dve", lambda e: e.tensor_tensor(rt[:, :, :], bc(pos[:, :], 2, [128, 18, 8]),
                                                  bc(invf[:, :], 1, [128, 18, 8]), ALU.mult),
                 reads=[r_pos, r_invf], writes=[r_rt])
            k.op("dve", lambda e: e.tensor_copy(rti[:, :, :], rt[:, :, :]), reads=[r_rt], writes=[r_rti])
            k.op("dve", lambda e: e.tensor_tensor(rfr[:, :, :], rt[:, :, :], rti[:, :, :], ALU.subtract),
                 reads=[r_rt, r_rti], writes=[r_rfr])
            k.op("act", lambda e: e.activation(rsin[:, :, :], rfr[:, :, :], AF.Sin, scale=TWO_PI_S),
                 reads=[r_rfr], writes=[r_rsin])
            k.op("act", lambda e: e.activation(rab[:, :, :], rfr[:, :, :], AF.Abs), reads=[r_rfr], writes=[r_rab])
            k.op("act", lambda e: e.activation(rcos[:, :, :], rab[:, :, :], AF.Sin, scale=-TWO_PI_S, bias=hpi[:, :]),
                 reads=[r_rab, r_hpi], writes=[r_rcos])

            stP = ExitStack()
            with stP:
                art, r_art = ld(stP, "art", art_d, [128, 32])
                ait, r_ait = ld(stP, "ait", ait_d, [128, 32])
                ldt, r_ldt = ld(stP, "ldt", ldt_d, [128, 32])
                btr, r_btr = ld(stP, "btr", btr_d, [128, 32, 16])
                bti, r_bti = ld(stP, "bti", bti_d, [128, 32, 16])
                ctr, r_ctr = ld(stP, "ctr", ctr_d, [128, 32, 16])
                cti, r_cti = ld(stP, "cti", cti_d, [128, 32, 16])
                kv, r_kv = ld(stP, "kv", kv_d, [128, 16])
                kvr, r_kvr = ld(stP, "kvr", kvr_d, [128, 16])
                maskM, r_maskM = ld(stP, "maskM", maskM_d, [128, 128])
                dt_, r_dt = SB(stP, "dt_", [128, 32])
                ardt, r_ardt = SB(stP, "ardt", [128, 32])
                aidt, r_aidt = SB(stP, "aidt", [128, 32])
                k.op("act", lambda e: e.activation(dt_[:, :], ldt[:, :], AF.Exp), reads=[r_ldt], writes=[r_dt])
                k.op("pool", lambda e: e.tensor_tensor(ardt[:, :], art[:, :], dt_[:, :], ALU.mult),
                     reads=[r_art, r_dt], writes=[r_ardt])
                k.op("pool", lambda e: e.tensor_tensor(aidt[:, :], ait[:, :], dt_[:, :], ALU.mult),
                     reads=[r_ait, r_dt], writes=[r_aidt])
                k.op("pool", lambda e: e.tensor_scalar(TH8[:, :], aidt[:, :], 8.0 / (2 * math.pi), None, ALU.mult),
                     reads=[r_aidt], writes=[r_TH8])

                def power_table(name, kvt, r_kvt):
                    Er, r_Er = SB(stP, name + "r", [128, 32, 16])
                    Ei, r_Ei = SB(stP, name + "i", [128, 32, 16])
                    mg, r_mg = SB(stP, name + "m", [128, 32, 16])
                    tq, r_tq = SB(stP, name + "t", [128, 32, 16])
                    tqi, r_tqi = SB(stP, name + "ti", [128, 32, 16], I32)
                    fr, r_fr = SB(stP, name + "f", [128, 32, 16])
                    ab, r_ab = SB(stP, name + "a", [128, 32, 16])
                    sh = [128, 32, 16]
                    k.op("pool", lambda e: e.tensor_tensor(mg[:, :, :], bc(ardt[:, :], 2, sh), bc(kvt[:, :], 1, sh), ALU.mult),
                         reads=[r_ardt, r_kvt], writes=[r_mg])
                    k.op("act", lambda e: e.activation(mg[:, :, :], mg[:, :, :], AF.Exp), reads=[r_mg], writes=[r_mg])
                    k.op("pool", lambda e: e.tensor_tensor(tq[:, :, :], bc(aidt[:, :], 2, sh), bc(kvt[:, :], 1, sh), ALU.mult),
                         reads=[r_aidt, r_kvt], writes=[r_tq])
                    k.op("dve", lambda e: e.tensor_scalar(tq[:, :, :], tq[:, :, :], 1.0 / (2 * math.pi), None, ALU.mult),
                         reads=[r_tq], writes=[r_tq])
                    k.op("dve", lambda e: e.tensor_copy(tqi[:, :, :], tq[:, :, :]), reads=[r_tq], writes=[r_tqi])
                    k.op("dve", lambda e: e.tensor_tensor(fr[:, :, :], tq[:, :, :], tqi[:, :, :], ALU.subtract),
                         reads=[r_tq, r_tqi], writes=[r_fr])
                    k.op("act", lambda e: e.activation(Ei[:, :, :], fr[:, :, :], AF.Sin, scale=TWO_PI_S),
                         reads=[r_fr], writes=[r_Ei])
                    k.op("act", lambda e: e.activation(ab[:, :, :], fr[:, :, :], AF.Abs), reads=[r_fr], writes=[r_ab])
                    k.op("act", lambda e: e.activation(Er[:, :, :], ab[:, :, :], AF.Sin, scale=-TWO_PI_S, bias=hpi[:, :]),
                         reads=[r_ab, r_hpi], writes=[r_Er])
                    k.op("pool", lambda e: e.tensor_tensor(Er[:, :, :], Er[:, :, :], mg[:, :, :], ALU.mult),
                         reads=[r_mg, r_Er], writes=[r_Er])
                    k.op("pool", lambda e: e.tensor_tensor(Ei[:, :, :], Ei[:, :, :], mg[:, :, :], ALU.mult),
                         reads=[r_mg, r_Ei], writes=[r_Ei])
                    return Er, Ei, r_Er, r_Ei, mg, r_mg

                Er, Ei, r_Er, r_Ei, mg, r_mg = power_table("E", kv, r_kv)
                Vr, Vi, r_Vr, r_Vi, _, _ = power_table("V", kvr, r_kvr)
                k.op("pool", lambda e: e.tensor_copy(R8[:, :], mg[:, :, 15]), reads=[r_mg], writes=[r_R8])
                k.op("pool", lambda e: e.tensor_copy(E4r[:, :], Er[:, :, 11]), reads=[r_Er], writes=[r_E4])
                k.op("pool", lambda e: e.tensor_copy(E4i[64:128, :], Ei[64:128, :, 11]), reads=[r_Ei], writes=[r_E4])
                k.op("pool", lambda e: e.tensor_scalar(E4i[0:64, :], Ei[0:64, :, 11], -1.0, None, ALU.mult),
                     reads=[r_Ei], writes=[r_E4])
                nr, r_nr = SB(stP, "nr", [128, 32]); den, r_den = SB(stP, "den", [128, 32])
                t1, r_t1 = SB(stP, "t1", [128, 32]); t2, r_t2 = SB(stP, "t2", [128, 32])
                cfr, r_cfr = SB(stP, "cfr", [128, 32]); cfi, r_cfi = SB(stP, "cfi", [128, 32])
                P_ = "pool"
                k.op(P_, lambda e: e.tensor_scalar(nr[:, :], Er[:, :, 8], -1.0, None, ALU.add), reads=[r_Er], writes=[r_nr])
                k.op(P_, lambda e: e.tensor_tensor(t1[:, :], art[:, :], art[:, :], ALU.mult), reads=[r_art], writes=[r_t1])
                k.op(P_, lambda e: e.tensor_tensor(t2[:, :], ait[:, :], ait[:, :], ALU.mult), reads=[r_ait], writes=[r_t2])
                k.op(P_, lambda e: e.tensor_tensor(den[:, :], t1[:, :], t2[:, :], ALU.add), reads=[r_t1, r_t2], writes=[r_den])
                k.op("dve", lambda e: e.reciprocal(den[:, :], den[:, :]), reads=[r_den], writes=[r_den])
                k.op(P_, lambda e: e.tensor_tensor(t1[:, :], nr[:, :], art[:, :], ALU.mult), reads=[r_nr, r_art], writes=[r_t1])
                k.op(P_, lambda e: e.tensor_tensor(t2[:, :], Ei[:, :, 8], ait[:, :], ALU.mult), reads=[r_Ei, r_ait], writes=[r_t2])
                k.op(P_, lambda e: e.tensor_tensor(t1[:, :], t1[:, :], t2[:, :], ALU.add), reads=[r_t1, r_t2], writes=[r_t1])
                k.op(P_, lambda e: e.tensor_tensor(cfr[:, :], t1[:, :], den[:, :], ALU.mult), reads=[r_t1, r_den], writes=[r_cfr])
                k.op(P_, lambda e: e.tensor_tensor(t1[:, :], Ei[:, :, 8], art[:, :], ALU.mult), reads=[r_Ei, r_art], writes=[r_t1])
                k.op(P_, lambda e: e.tensor_tensor(t2[:, :], nr[:, :], ait[:, :], ALU.mult), reads=[r_nr, r_ait], writes=[r_t2])
                k.op(P_, lambda e: e.tensor_tensor(t1[:, :], t1[:, :], t2[:, :], ALU.subtract), reads=[r_t1, r_t2], writes=[r_t1])
                k.op(P_, lambda e: e.tensor_tensor(cfi[:, :], t1[:, :], den[:, :], ALU.mult), reads=[r_t1, r_den], writes=[r_cfi])
                Br, r_Br = SB(stP, "Br", [128, 32, 16]); Bi, r_Bi = SB(stP, "Bi", [128, 32, 16])
                u1, r_u1 = SB(stP, "u1", [128, 32, 16]); u2, r_u2 = SB(stP, "u2", [128, 32, 16])
                s3 = [128, 32, 16]
                k.op(P_, lambda e: e.tensor_tensor(u1[:, :, :], btr[:, :, :], bc(cfr[:, :], 2, s3), ALU.mult), reads=[r_btr, r_cfr], writes=[r_u1])
                k.op(P_, lambda e: e.tensor_tensor(u2[:, :, :], bti[:, :, :], bc(cfi[:, :], 2, s3), ALU.mult), reads=[r_bti, r_cfi], writes=[r_u2])
                k.op(P_, lambda e: e.tensor_tensor(Br[:, :, :], u1[:, :, :], u2[:, :, :], ALU.subtract), reads=[r_u1, r_u2], writes=[r_Br])
                k.op(P_, lambda e: e.tensor_tensor(u1[:, :, :], bti[:, :, :], bc(cfr[:, :], 2, s3), ALU.mult), reads=[r_bti, r_cfr], writes=[r_u1])
                k.op(P_, lambda e: e.tensor_tensor(u2[:, :, :], btr[:, :, :], bc(cfi[:, :], 2, s3), ALU.mult), reads=[r_btr, r_cfi], writes=[r_u2])
                k.op(P_, lambda e: e.tensor_tensor(Bi[:, :, :], u1[:, :, :], u2[:, :, :], ALU.add), reads=[r_u1, r_u2], writes=[r_Bi])

                s4 = [128, 8, 8, 16]
                def uxf(i):
                    v = UX[:, i * 2048:(i + 1) * 2048].bitcast(F32)
                    return v.rearrange("p (g s j) -> p g s j", g=8, s=8), Res()
                (p1, r_p1), (p2, r_p2), (p3, r_p3), (p4, r_p4) = uxf(0), uxf(1), uxf(2), uxf(3)
                (Zs, r_Zs), (Zd, r_Zd), (Ws, r_Ws) = uxf(4), uxf(5), uxf(6)
                ppA, r_ppA = PS(stP, "ppA", [128, 4, 128]); ppB, r_ppB = PS(stP, "ppB", [128, 4, 128])
                ppC, r_ppC = PS(stP, "ppC", [128, 4, 128])
                LO = slice(0, 64); HI = slice(64, 128)
                for gb in range(4):
                    gs = slice(gb * 8, gb * 8 + 8)

                    def cprod(Tr, Ti, r_Tr, r_Ti, ksl, Xr, Xi, r_Xr, r_Xi):
                        er = bc(Tr[:, gs, ksl], 3, s4); ei = bc(Ti[:, gs, ksl], 3, s4)
                        xr = bc(Xr[:, gs, :], 2, s4); xi = bc(Xi[:, gs, :], 2, s4)
                        k.op("dve", lambda e: e.tensor_tensor(p1[:, :, :, :], er, xr, ALU.mult), reads=[r_Tr, r_Xr], writes=[r_p1])
                        k.op("dve", lambda e: e.tensor_tensor(p2[:, :, :, :], ei, xi, ALU.mult), reads=[r_Ti, r_Xi], writes=[r_p2])
                        k.op("dve", lambda e: e.tensor_tensor(p3[:, :, :, :], er, xi, ALU.mult), reads=[r_Tr, r_Xi], writes=[r_p3])
                        k.op(P_, lambda e: e.tensor_tensor(p4[:, :, :, :], ei, xr, ALU.mult), reads=[r_Ti, r_Xr], writes=[r_p4])

                    cprod(Vr, Vi, r_Vr, r_Vi, slice(1, 9), Br, Bi, r_Br, r_Bi)
                    k.op("dve", lambda e: e.tensor_tensor(Zs[LO], p1[LO], p2[LO], ALU.subtract), reads=[r_p1, r_p2], writes=[r_Zs])
                    k.op("dve", lambda e: e.tensor_tensor(Zs[HI], p3[HI], p4[HI], ALU.add), reads=[r_p3, r_p4], writes=[r_Zs])
                    k.op("dve", lambda e: e.tensor_tensor(Zd[LO], p3[LO], p4[LO], ALU.add), reads=[r_p3, r_p4], writes=[r_Zd])
                    k.op("dve", lambda e: e.tensor_tensor(Zd[HI], p2[HI], p1[HI], ALU.subtract), reads=[r_p1, r_p2], writes=[r_Zd])
                    cprod(Er, Ei, r_Er, r_Ei, slice(0, 8), ctr, cti, r_ctr, r_cti)
                    k.op("dve", lambda e: e.tensor_tensor(Ws[LO], p1[LO], p2[LO], ALU.subtract), reads=[r_p1, r_p2], writes=[r_Ws])
                    k.op("dve", lambda e: e.scalar_tensor_tensor(Ws[HI], p3[HI], -1.0, p4[HI], ALU.mult, ALU.subtract),
                         reads=[r_p3, r_p4], writes=[r_Ws])
                    for hb in range(2):
                        for gi in range(4):
                            g8 = hb * 4 + gi
                            zs = Zs[:, g8, :, :].rearrange("p s j -> p (s j)")
                            zd = Zd[:, g8, :, :].rearrange("p s j -> p (s j)")
                            ws = Ws[:, g8, :, :].rearrange("p s j -> p (s j)")
                            k.op("pe", lambda e: e.transpose(ppA[:, gi, :], zs, identf[:, :]),
                                 reads=[r_Zs, r_identf], writes=[r_ppA])
                            k.op("pe", lambda e: e.transpose(ppB[:, gi, :], zd, identf[:, :]),
                                 reads=[r_Zd, r_identf], writes=[r_ppB])
                            k.op("pe", lambda e: e.matmul(ppC[:, gi, :], zs, ws, start=True, stop=True),
                                 reads=[r_Zs, r_Ws], writes=[r_ppC])
                        g0 = gb * 8 + hb * 4
                        k.op("act", lambda e: e.copy(Pbf[:, g0:g0 + 4, :], ppA[:, :, :]), reads=[r_ppA], writes=[r_Pbf])
                        k.op("act", lambda e: e.copy(Pdbf[:, g0:g0 + 4, :], ppB[:, :, :]), reads=[r_ppB], writes=[r_Pdbf])
                        k.op("dve", lambda e: e.tensor_tensor(Mbf[:, g0:g0 + 4, :], ppC[:, :, :],
                                                              bc(maskM[:, :], 1, [128, 4, 128]), ALU.mult),
                             reads=[r_ppC, r_maskM], writes=[r_Mbf])
                    cprod(Er, Ei, r_Er, r_Ei, slice(8, 16), ctr, cti, r_ctr, r_cti)
                    qv = Qbf[:, gs, :].rearrange("p g (t c) -> p g t c", c=16)
                    k.op("dve", lambda e: e.tensor_tensor(qv[LO], p1[LO], p2[LO], ALU.subtract), reads=[r_p1, r_p2], writes=[r_Qbf])
                    k.op("dve", lambda e: e.scalar_tensor_tensor(qv[HI], p3[HI], -1.0, p4[HI], ALU.mult, ALU.subtract),
                         reads=[r_p3, r_p4], writes=[r_Qbf])
                k.barrier()
            for c in range(8):
                k.op("dve", lambda e: e.tensor_scalar(win[:, c, :], win[:, c, :], gpre[:, c:c + 1], None, ALU.mult),
                     reads=[r_gpre, r_win], writes=[r_win])
            UTs, r_UTs = SB(stA, "UTs", [128, 4, 4, 16], BF16)
            xt = [SB(stA, "xt%d" % i, [128, 1024]) for i in range(2)]
            junk, r_junk = SB(stA, "junk", [128, 1024], BF16)
            ss = [SB(stA, "ss%d" % i, [128, 1]) for i in range(2)]
            xn = [SB(stA, "xn%d" % i, [128, 1024], BF16) for i in range(2)]
            hT = [SB(stA, "hT%d" % i, [128, 8, 128], BF16) for i in range(2)]
            qkf = [SB(stA, "qkf%d" % i, [128, 896]) for i in range(2)]
            rtmp, r_rtmp = SB(stA, "rtmp", [128, 4, 12, 8])
            qkb, r_qkb = SB(stA, "qkb", [128, 768], BF16)
            tpp = [PS(stA, "tpp%d" % i, [128, 8, 128], BF16) for i in range(2)]
            pq, r_pq = PS(stA, "pq", [128, 512]); pkv, r_pkv = PS(stA, "pkv", [128, 384])
            pu, r_pu = PS(stA, "pu", [128, 4, 128]); pqt, r_pqt = PS(stA, "pqt", [128, 6, 128], BF16)

            tiles = [("p", i) for i in range(NTILE_P)] + [("m", i) for i in range(NTILE_M)] + [("s", 0)]
            def tile_body(it, kind, ti):
                b2 = it % 2
                np_ = 64 if kind == "s" else 128
                PSL = slice(0, np_)
                src = {"p": xp, "m": xm, "s": xs}[kind]
                (xt_, r_xt), (ss_, r_ss), (xn_, r_xn), (hT_, r_hT), (qkf_, r_qkf), (tp_, r_tp) = \
                    xt[b2], ss[b2], xn[b2], hT[b2], qkf[b2], tpp[b2]
                k.dma("sp", xt_[PSL, :], src[ti * 128: ti * 128 + np_, :], writes=[r_xt])
                k.op("act", lambda e: e.activation(junk[PSL, :], xt_[PSL, :], AF.Square, accum_out=ss_[PSL, :]),
                     reads=[r_xt], writes=[r_junk, r_ss])
                k.op("act", lambda e: e.activation(ss_[PSL, :], ss_[PSL, :], AF.Sqrt, bias=EPS, scale=1.0 / 1024),
                     reads=[r_ss], writes=[r_ss])
                k.op("dve", lambda e: e.reciprocal(ss_[PSL, :], ss_[PSL, :]), reads=[r_ss], writes=[r_ss])
                yield
                k.op("act", lambda e: e.activation(xn_[PSL, :], xt_[PSL, :], AF.Copy, scale=ss_[PSL, :]),
                     reads=[r_xt, r_ss], writes=[r_xn])
                for c in range(8):
                    k.op("pe", lambda e: e.transpose(tp_[:, c, PSL], xn_[PSL, c * 128:(c + 1) * 128], ident[PSL, PSL]),
                         reads=[r_xn, r_ident], writes=[r_tp], sig=SG("A_tp", c == 7))
                k.op("dve", lambda e: e.tensor_copy(hT_[:, :, PSL], tp_[:, :, PSL]), reads=[r_tp], writes=[r_hT])
                yield
                for ch in range(4):
                    for c in range(8):
                        k.op("pe", lambda e: e.matmul(pu[:, ch, PSL], win[:, c, 896 + ch * 128: 896 + (ch + 1) * 128],
                                                      hT_[:, c, PSL], start=(c == 0), stop=(c == 7)),
                             reads=[r_win, r_hT], writes=[r_pu], sig=SG("A_u", c == 7 and ch == 3), chain=CH("A_u", c > 0))
                if kind != "p":
                    for c in range(8):
                        k.op("pe", lambda e: e.matmul(pq[PSL, :], hT_[:, c, PSL], win[:, c, 0:512], start=(c == 0), stop=(c == 7)),
                             reads=[r_win, r_hT], writes=[r_pq], sig=SG("A_q", c == 7), chain=CH("A_q", c > 0))
                    for c in range(8):
                        k.op("pe", lambda e: e.matmul(pkv[PSL, :], hT_[:, c, PSL], win[:, c, 512:896], start=(c == 0), stop=(c == 7)),
                             reads=[r_win, r_hT], writes=[r_pkv], sig=SG("A_kv", c == 7), chain=CH("A_kv", c > 0))
                yield
                if kind == "s":
                    k.op("dve", lambda e: e.tensor_copy(UTs[:, :, :, :].rearrange("p c t b -> p c b t"),
                                                        pu[:, :, 0:64].rearrange("p c (b t) -> p c b t", t=4)),
                         reads=[r_pu], writes=[r_UTs])
                else:
                    n0 = (ti * 16) if kind == "p" else (NPRE + ti * 16)
                    k.op("dve", lambda e: e.tensor_copy(UT[:, :, :, n0:n0 + 16].rearrange("p c s n -> p c n s"),
                                                        pu[:, :, :].rearrange("p c (n s) -> p c n s", s=8)),
                         reads=[r_pu], writes=[r_UT])
                if kind == "p":
                    return
                k.op("act", lambda e: e.copy(qkf_[PSL, 0:512], pq[PSL, :]), reads=[r_pq], writes=[r_qkf])
                k.op("act", lambda e: e.copy(qkf_[PSL, 512:896], pkv[PSL, :]), reads=[r_pkv], writes=[r_qkf])
                yield
                pi = 17 if kind == "s" else ti
                qv = qkf_[PSL, 0:768].rearrange("p (h d) -> p h d", d=64)
                cs_ = bc(rcos[PSL, pi, :], 1, [np_, 12, 8]); sn_ = bc(rsin[PSL, pi, :], 1, [np_, 12, 8])
                k.op("dve", lambda e: e.tensor_tensor(rtmp[PSL, 0], qv[:, :, 0:8], cs_, ALU.mult), reads=[r_qkf, r_rcos], writes=[r_rtmp])
                k.op("dve", lambda e: e.tensor_tensor(rtmp[PSL, 1], qv[:, :, 8:16], sn_, ALU.mult), reads=[r_qkf, r_rsin], writes=[r_rtmp])
                k.op("pool", lambda e: e.tensor_tensor(rtmp[PSL, 2], qv[:, :, 8:16], cs_, ALU.mult), reads=[r_qkf, r_rcos], writes=[r_rtmp])
                k.op("pool", lambda e: e.tensor_tensor(rtmp[PSL, 3], qv[:, :, 0:8], sn_, ALU.mult), reads=[r_qkf, r_rsin], writes=[r_rtmp])
                k.op("dve", lambda e: e.tensor_tensor(qv[:, :, 0:8], rtmp[PSL, 0], rtmp[PSL, 1], ALU.subtract), reads=[r_rtmp], writes=[r_qkf])
                k.op("dve", lambda e: e.tensor_tensor(qv[:, :, 8:16], rtmp[PSL, 2], rtmp[PSL, 3], ALU.add), reads=[r_rtmp], writes=[r_qkf])
                k.op("act", lambda e: e.copy(qkb[PSL, :], qkf_[PSL, 0:768]), reads=[r_qkf], writes=[r_qkb])
                k.op("pool", lambda e: e.tensor_copy(Vall[PSL, pi, :, 0:64], qkf_[PSL, 768:896].rearrange("p (a d) -> p a d", d=64)),
                     reads=[r_qkf], writes=[r_V])
                yield
                for j in range(2):
                    k.op("pe", lambda e: e.transpose(pqt[:, j, PSL], qkb[PSL, 512 + j * 128: 512 + (j + 1) * 128], ident[PSL, PSL]),
                         reads=[r_qkb, r_ident], writes=[r_pqt], sig=SG("A_kt", j == 1))
                k.op("dve", lambda e: e.tensor_copy(kT[:, pi, :, PSL], pqt[:, 0:2, PSL]), reads=[r_pqt], writes=[r_kT])
                if not (kind == "m" and ti == 0):
                    qi = 16 if kind == "s" else ti - 1
                    for j in range(4):
                        k.op("pe", lambda e: e.transpose(pqt[:, 2 + j, PSL], qkb[PSL, j * 128:(j + 1) * 128], ident[PSL, PSL]),
                             reads=[r_qkb, r_ident], writes=[r_pqt], sig=SG("A_qt", j == 3))
                    k.op("dve", lambda e: e.tensor_copy(qT[:, qi, :, PSL], pqt[:, 2:6, PSL]), reads=[r_pqt], writes=[r_qT])
                if kind == "m" and ti == NTILE_M - 1:
                    k.dma("sp", kwin[:, 0:64], qkf_[:, 512:576], reads=[r_qkf])
                    k.dma("sp", kwin[:, 64:128], qkf_[:, 640:704], reads=[r_qkf])
                    k.dma("sp", vwin[:, :], qkf_[:, 768:896], reads=[r_qkf])
                if kind == "s":
                    k.dma("sp", knew[:, 0:64], qkf_[0:64, 512:576], reads=[r_qkf])
                    k.dma("sp", knew[:, 64:128], qkf_[0:64, 640:704], reads=[r_qkf])
                    k.dma("sp", vnew[:, :], qkf_[0:64, 768:896], reads=[r_qkf])
            run_staged([tile_body(it, kind, ti) for it, (kind, ti) in enumerate(tiles)], 6, order=[3, 5, 2, 4, 1, 0])
            r_Udl = [Res() for _ in range(4)]; r_Usdl = [Res() for _ in range(4)]
            for ch in range(4):
                k.dma("sp", Ud[ch], UT[:, ch, :, :], reads=[r_UT], writes=[r_Udl[ch]])
                k.dma("sp", Usd[ch], UTs[:, ch, :, :], reads=[r_UTs], writes=[r_Usdl[ch]])
            k.barrier()

        stB = ExitStack()
        with stB:
            k.wait_all("sp", [r_UT])
            Uv0 = Ud.rearrange("c (g j) s n -> s j (c g) n", j=16)
            for s in range(8):
                k.dma("sp", USJ[16 * s:16 * s + 16, :, :], Uv0[s], reads=r_Udl, writes=[r_USJl[s]])
            mask1, r_m1 = SB(stB, "mask1f", [128, 128]); maskp, r_mp = SB(stB, "maskpf", [128, 128])
            maskc, r_mc = SB(stB, "maskcf", [128, 128]); masks, r_ms = SB(stB, "masksf", [128, 64])
            k.dma("sp", mask1[:, :], mask1_d, writes=[r_m1]); k.dma("sp", maskp[:, :], maskp_d, writes=[r_mp])
            k.dma("sp", maskc[:, :], maskc_d, writes=[r_mc]); k.dma("sp", masks[:, :], masks_d, writes=[r_ms])
            mk = {}
            for nm, (t_, r_) in {"1": (mask1, r_m1), "p": (maskp, r_mp), "c": (maskc, r_mc)}.items():
                tb, rb = SB(stB, "mb" + nm, [128, 128], BF16)
                k.op("dve", lambda e: e.tensor_copy(tb[:, :], t_[:, :]), reads=[r_], writes=[rb])
                mk[nm] = (tb, rb)
            msb, r_msb = SB(stB, "msb", [128, 64], BF16)
            k.op("dve", lambda e: e.tensor_copy(msb[:, :], masks[:, :]), reads=[r_ms], writes=[r_msb])
            esink, r_esink = ld(stB, "esink", sink_d, [128, 8])
            k.op("act", lambda e: e.activation(esink[:, :], esink[:, :], AF.Exp), reads=[r_esink], writes=[r_esink])
            PT = [SB(stB, "PT%d" % i, [128, 2, 2, 2, 128], BF16) for i in range(4)]
            LT = [PS(stB, "LT%d" % i, [128, 2, 2, 2, 128]) for i in range(2)]
            Op = [PS(stB, "Op%d" % i, [128, 4, 65]) for i in range(2)]
            ptr, r_ptr = PS(stB, "ptr", [128, 4, 128], BF16)
            attl = [SB(stB, "att%d" % i, [128, 8, 64]) for i in range(2)]
            attb, r_attb = SB(stB, "attb", [128, 512], BF16)
            dnl = [SB(stB, "dn%d" % i, [128, 8]) for i in range(2)]
            ssa, r_ssa = SB(stB, "ssa", [128, 1])
            junkb, r_junkb = SB(stB, "junkb", [128, 512], BF16)

            def attn_evac(npart, Ops, par):
                PSL = slice(0, npart)
                (att, r_att), (dn, r_dn) = attl[par], dnl[par]
                for kvh in range(2):
                    o_, r_o = Ops[kvh]
                    hs = slice(kvh * 4, kvh * 4 + 4)
                    k.op("dve", lambda e: e.tensor_tensor(dn[PSL, hs], o_[PSL, :, 64], esink[PSL, hs], ALU.add),
                         reads=[r_o, r_esink], writes=[r_dn])
                    k.op("dve", lambda e: e.reciprocal(dn[PSL, hs], dn[PSL, hs]), reads=[r_dn], writes=[r_dn])
                    k.op("dve", lambda e: e.tensor_tensor(att[PSL, hs, :], o_[PSL, :, 0:64], bc(dn[PSL, hs], 2, [npart, 4, 64]), ALU.mult),
                         reads=[r_o, r_dn], writes=[r_att])

            def attn_norm(npart, col0, par):
                PSL = slice(0, npart)
                (att, r_att) = attl[par]
                av = att[PSL, :, :].rearrange("p h d -> p (h d)")
                k.op("act", lambda e: e.activation(junkb[PSL, :], av, AF.Square, accum_out=ssa[PSL, :]),
                     reads=[r_att], writes=[r_junkb, r_ssa])
                k.op("act", lambda e: e.activation(ssa[PSL, :], ssa[PSL, :], AF.Ln, bias=epsb[PSL, :], scale=1.0 / 512),
                     reads=[r_ssa, r_epsb], writes=[r_ssa])
                k.op("act", lambda e: e.activation(ssa[PSL, :], ssa[PSL, :], AF.Exp, scale=-0.5), reads=[r_ssa], writes=[r_ssa])
                k.op("act", lambda e: e.activation(attb[PSL, :], av, AF.Copy, scale=ssa[PSL, :]),
                     reads=[r_att, r_ssa], writes=[r_attb])
                for j in range(4):
                    k.op("pe", lambda e: e.transpose(ptr[:, j, PSL], attb[PSL, j * 128:(j + 1) * 128], ident[PSL, PSL]),
                         reads=[r_attb, r_ident], writes=[r_ptr], sig=SG("B_tr", j == 3))
                k.op("dve", lambda e: e.tensor_copy(catT[:, 0:4, col0:col0 + npart], ptr[:, :, PSL]),
                     reads=[r_ptr], writes=[r_catT])

            def finish_attention(npart, Ops, col0):
                attn_evac(npart, Ops, 0)
                attn_norm(npart, col0, 0)

            def attn_block(blk):
                pts = []
                for kvh in range(2):
                    (lt, r_lt) = LT[kvh]
                    (pt, r_pt) = PT[(blk % 2) * 2 + kvh]
                    pts.append((pt, r_pt))
                    for kb in range(2):
                        for hf in range(2):
                            hs_ = slice(hf * 64, hf * 64 + 64)
                            k.op("pe", lambda e: e.matmul(lt[:, hf, kb, :, :], kT[hs_, blk - 1 + kb, kvh, :],
                                                          qT[hs_, blk - 1, 2 * kvh:2 * kvh + 2, :], start=True, stop=True),
                                 reads=[r_kT, r_qT], writes=[r_lt], sig=SG("B_lg", kb == 1 and hf == 1))
                    k.op("act", lambda e: e.activation(pt[:, :, :, :, :].rearrange("p a b c q -> p (a b c q)"),
                                                       lt.rearrange("p a b c q -> p (a b c q)"), AF.Exp, scale=0.125),
                         reads=[r_lt], writes=[r_pt])
                    for kb in range(2):
                        mt, r_mt = mk["c"] if kb == 1 else (mk["1"] if blk == 1 else mk["p"])
                        pv = pt[:, :, kb, :, :]
                        k.op("dve",
                             lambda e: e.tensor_tensor(pv, pv, mt[:, :].unsqueeze(1).unsqueeze(1).to_broadcast([128, 2, 2, 128]), ALU.mult),
                             reads=[r_mt, r_pt], writes=[r_pt])
                yield
                for kvh in range(2):
                    (pt, r_pt) = pts[kvh]
                    o_, r_o = Op[kvh]
                    for j in range(2):
                        for hf in range(2):
                            for kb in range(2):
                                k.op("pe", lambda e: e.matmul(o_[:, 2 * j + hf, :], pt[:, hf, kb, j, :], Vall[:, blk - 1 + kb, kvh, :],
                                                              start=(kb == 0), stop=(kb == 1)),
                                     reads=[r_pt, r_V], writes=[r_o], sig=SG("B_pv", j == 1 and hf == 1 and kb == 1), chain=CH("B_pv", kb == 1))
                yield
                attn_evac(128, Op, blk % 2)
                yield
                attn_norm(128, blk * 128, blk % 2)

            ab_ = [attn_block(blk) for blk in range(1, 17)]

            def aseg(i):
                try:
                    next(ab_[i])
                except StopIteration:
                    pass

            aseg(0)
            for i in range(16):
                if i + 1 < 16:
                    aseg(i + 1)
                aseg(i)
                if i >= 1:
                    aseg(i - 1)
                aseg(i)
            aseg(15)

            ckb, r_ckb = SB(stB, "ckb", [128, 16, 256], BF16)
            k.dma("pool", ckb[:, :, :], ckd_d.rearrange("b w c -> w b c"), writes=[r_ckb])
            Vc, r_Vc = SB(stB, "Vc", [128, 16, 2, 65], BF16)
            k.op("pool", lambda e: e.memset(Vc[:, :, :, :], 1.0), writes=[r_Vc])
            for a_ in range(2):
                k.dma("pool", Vc[:, :, a_, 0:64], cv_d[:, :, a_ * 64:(a_ + 1) * 64].rearrange("b w d -> w b d"), writes=[r_Vc])
            r_kcs = Res()
            k.dma("sp", kc_s[:, :, :], ck_d[:, 4:128, :], writes=[r_kcs])
            k.dma("sp", vc_s[:, :, :], cv_d[:, 4:128, :], writes=[r_kcs])
            PTn, r_PTn = SB(stB, "PTn", [128, 2, 2, 2, 64], BF16)
            k.op("pool", lambda e: e.memset(PTn[:, :, :, :, :], 0.0), writes=[r_PTn])
            ZPA, r_ZPA = SB(stB, "ZPA", [128, 16, 8, 64], BF16)
            k.op("pool", lambda e: e.memset(ZPA[:, :, :, :], 0.0), writes=[r_ZPA])
            kTcA, r_kTcA = SB(stB, "kTcA", [128, 16, 2, 128], BF16)
            ptc_all, r_ptc = PS(stB, "ptc", [128, 8, 128], BF16)
            ltn, r_ltn = LT[0]
            for kvh in range(2):
                for hf in range(2):
                    hs_ = slice(hf * 64, hf * 64 + 64)
                    for j in range(2):
                        k.op("pe", lambda e: e.matmul(ltn[0:64, hf, kvh, j, 0:64], kT[hs_, 17, kvh, 0:64],
                                                      qT[hs_, 16, 2 * kvh + j, 0:64], start=True, stop=True),
                             reads=[r_kT, r_qT], writes=[r_ltn])
            k.op("act", lambda e: e.activation(PTn[0:64].rearrange("p a b c q -> p (a b c) q"),
                                               ltn.rearrange("p a b c q -> p (a b c) q")[0:64, :, 0:64], AF.Exp, scale=0.125),
                 reads=[r_ltn], writes=[r_PTn])
            pnv = PTn[0:64].rearrange("p a b c q -> p (a b c) q")
            k.op("dve", lambda e: e.tensor_tensor(pnv, pnv, bc(msb[0:64, :], 1, [64, 8, 64]), ALU.mult),
                 reads=[r_msb, r_PTn], writes=[r_PTn])
            for kvh in range(2):
                o_, r_o = Op[kvh]
                for j in range(2):
                    for hf in range(2):
                        k.op("pe", lambda e: e.matmul(o_[0:64, 2 * j + hf, :], PTn[:, hf, kvh, j, :], Vall[:, 17, kvh, :],
                                                      start=(j == 0 and hf == 0), stop=False, skip_group_check=True),
                             reads=[r_PTn, r_V], writes=[r_o])
            ltc, r_ltc = LT[1]
            for r4 in range(4):
                for bi in range(4):
                    b = r4 * 4 + bi
                    for j in range(2):
                        k.op("pe", lambda e: e.transpose(ptc_all[:, 2 * bi + j, :], ckb[:, b, j * 128:(j + 1) * 128], ident[:, :]),
                             reads=[r_ckb, r_ident], writes=[r_ptc], sig=(bi == 3 and j == 1))
                k.op("dve", lambda e: e.tensor_copy(kTcA[:, r4 * 4:r4 * 4 + 4, :, :].rearrange("p b j w -> p (b j) w"), ptc_all[:, :, :]),
                     reads=[r_ptc], writes=[r_kTcA])
            for b in range(16):
                for kvh in range(2):
                    for hf in range(2):
                        hs_ = slice(hf * 64, hf * 64 + 64)
                        for j in range(2):
                            k.op("pe", lambda e: e.matmul(ltc[:, hf, kvh, j, 4 * b:4 * b + 4], kTcA[hs_, b, kvh, :],
                                                          qT[hs_, 16, 2 * kvh + j, 4 * b:4 * b + 4], start=True, stop=True),
                                 reads=[r_kTcA, r_qT], writes=[r_ltc], sig=(b == 15 and kvh == 1 and hf == 1 and j == 1))
            lsrc = ltc.rearrange("p a b c q -> p (a b c) q")[:, :, 0:64].rearrange("p h (b t) -> p b h t", t=4)
            zdst = bass.AP(tensor=ZPA[:, :, :, :].tensor, offset=ZPA[:, :, :, :].offset,
                           ap=[list(ZPA[:, :, :, :].ap[0]), [8 * 64 + 4, 16], [64, 8], [1, 4]])
            k.op("act", lambda e: e.activation(zdst, lsrc, AF.Exp, scale=0.125), reads=[r_ltc], writes=[r_ZPA])
            mpb, r_mpb = mk["p"]
            k.op("dve", lambda e: e.tensor_tensor(zdst, zdst, mpb[:, 0:4].unsqueeze(1).unsqueeze(1).to_broadcast([128, 16, 8, 4]), ALU.mult),
                 reads=[r_mpb, r_ZPA], writes=[r_ZPA])
            for b in range(16):
                for kvh in range(2):
                    o_, r_o = Op[kvh]
                    for j in range(2):
                        for hf in range(2):
                            k.op("pe", lambda e: e.matmul(o_[0:64, 2 * j + hf, :], ZPA[:, b, (hf * 2 + kvh) * 2 + j, :], Vc[:, b, kvh, :],
                                                          start=False, stop=(b == 15), skip_group_check=True),
                                 reads=[r_ZPA, r_Vc], writes=[r_o], sig=(b == 15 and kvh == 1 and j == 1 and hf == 1), chain=True)
            finish_attention(64, Op, 2176)
            k.wait_all("sp", [r_kcs])
            k.barrier()
        stAB.close()

        stC = ExitStack()
        with stC:
            USJs, r_USJs = SB(stC, "USJs", [128, 32, 16], BF16)
            k.op("pool", lambda e: e.memset(USJs[:, :, :], 0.0), writes=[r_USJs])
            r_USJsl = [Res() for _ in range(4)]
            Uv = Ud.rearrange("c (g j) s n -> s j (c g) n", j=16)
            Usv = Usd.rearrange("c (g j) t b -> t j (c g) b", j=16)
            for t in range(4):
                k.dma("sp", USJs[64 + 16 * t:64 + 16 * t + 16, :, :], Usv[t], reads=r_Usdl + [r_USJs], writes=[r_USJsl[t]])
            iot, r_iot = ld(stC, "iot", iota_d, [128, NT])
            YSC, r_YSC = SB(stC, "YSC", [128, 32, NMAIN], BF16)
            Hend, r_Hend = SB(stC, "Hend", [128, 32])
            NB = 2
            W = {}
            for nm in ("SA", "SB", "cs", "sn", "ab", "tq", "fr", "m1", "m2", "S1", "S2", "G1", "G2", "h1", "h2"):
                W[nm] = [SB(stC, "w_%s%d" % (nm, i), [128, NT]) for i in range(NB)]
            Wti = [SB(stC, "w_ti%d" % i, [128, NT], I32) for i in range(NB)]
            Hb = [SB(stC, "Hb%d" % i, [128, NMAIN], BF16) for i in range(NB)]
            pSA = [PS(stC, "pSA%d" % i, [128, 2, 512]) for i in range(1)]
            pSB = [PS(stC, "pSB%d" % i, [128, 2, 512]) for i in range(1)]
            pY = [PS(stC, "pY%d" % i, [128, NMAIN]) for i in range(2)]
            HALF = NT // 2
            NO = NMAIN + 1
            r_YSCb = [Res() for _ in range(4)]
            r_Ydb = [[Res() for _ in range(8)] for _ in range(4)]

            h0s, r_h0s = ld(stC, "h0s", h0s_d, [128, 32, 16])
            h0d, r_h0d = ld(stC, "h0d", h0d_d, [128, 32, 16])
            h0b, r_h0b = SB(stC, "h0b", [128, 32, 16], BF16)
            k.op("pool", lambda e: e.tensor_copy(h0b[:, :, :], h0s[:, :, :]), reads=[r_h0s], writes=[r_h0b])
            pys, r_pys = PS(stC, "pys", [128, 512]); pss, r_pss = PS(stC, "pss", [128, 512])
            pysv = pys[0:64, :].rearrange("p (g b) -> p g b", b=16)
            pssv = pss[:, :].rearrange("p (g b) -> p g b", b=16)
            r_G2h = [(Res(), Res()) for _ in range(NB)]
            sgn2pi, r_sgn = SB(stC, "sgn2pi", [128, 1])
            k.op("pool", lambda e: e.memset(sgn2pi[0:64, :], -TWO_PI_S), writes=[r_sgn])
            k.op("pool", lambda e: e.memset(sgn2pi[64:128, :], TWO_PI_S), writes=[r_sgn])

            def ssm_group(g):
                i2 = g % NB
                (sa, r_sa), (sb_, r_sb) = W["SA"][i2], W["SB"][i2]
                (tq, r_tq), (ti_, r_ti), (fr, r_fr), (ab, r_ab) = W["tq"][i2], Wti[i2], W["fr"][i2], W["ab"][i2]
                (cs, r_cs), (sn, r_sn) = W["cs"][i2], W["sn"][i2]
                k.op("act", lambda e: e.activation(tq[:, :], iot[:, :], AF.Copy, scale=TH8[:, g:g + 1]),
                     reads=[r_iot, r_TH8], writes=[r_tq])
                (psa, r_psa), (psb, r_psb) = pSA[0], pSB[0]
                for h2 in range(2):
                    k.op("pe", lambda e: e.matmul(psa[:, h2, 0:HALF], Pbf[:, g, :], USJ[:, g, h2 * HALF:(h2 + 1) * HALF], start=True, stop=True),
                         reads=[r_Pbf] + r_USJl, writes=[r_psa])
                    k.op("pe", lambda e: e.matmul(psb[:, h2, 0:HALF], Pdbf[:, g, :], USJ[:, g, h2 * HALF:(h2 + 1) * HALF], start=True, stop=True),
                         reads=[r_Pdbf] + r_USJl, writes=[r_psb])
                k.op("act", lambda e: e.copy(sa[:, :].rearrange("p (a n) -> p a n", a=2), psa[:, :, 0:HALF]), reads=[r_psa], writes=[r_sa])
                k.op("act", lambda e: e.copy(sb_[:, :].rearrange("p (a n) -> p a n", a=2), psb[:, :, 0:HALF]), reads=[r_psb], writes=[r_sb])
                yield
                k.op("dve", lambda e: e.tensor_copy(ti_[:, :], tq[:, :]), reads=[r_tq], writes=[r_ti])
                k.op("dve", lambda e: e.tensor_tensor(fr[:, :], tq[:, :], ti_[:, :], ALU.subtract), reads=[r_tq, r_ti], writes=[r_fr])
                k.op("act", lambda e: e.activation(sn[:, :], fr[:, :], AF.Sin, scale=TWO_PI_S), reads=[r_fr], writes=[r_sn])
                k.op("act", lambda e: e.activation(ab[:, :], fr[:, :], AF.Abs), reads=[r_fr], writes=[r_ab])
                k.op("act", lambda e: e.activation(cs[:, :], ab[:, :], AF.Sin, scale=-TWO_PI_S, bias=hpi[:, :]),
                     reads=[r_ab, r_hpi], writes=[r_cs])
                (m1, r_m1_), (m2, r_m2_), (S1, r_S1), (S2, r_S2) = W["m1"][i2], W["m2"][i2], W["S1"][i2], W["S2"][i2]
                k.op("act", lambda e: e.activation(m2[:, NPRE - 1:NT], fr[:, NPRE - 1:NT], AF.Sin, scale=sgn2pi[:, 0:1]),
                     reads=[r_fr, r_sgn], writes=[r_m2_])
                (G1, r_G1), (G2, r_G2) = W["G1"][i2], W["G2"][i2]
                yield
                k.op("dve", lambda e: e.tensor_tensor(m1[:, :], sb_[:, :], sn[:, :], ALU.mult), reads=[r_sb, r_sn], writes=[r_m1_])
                k.op("dve", lambda e: e.tensor_tensor(S1[:, :], sa[:, :], cs[:, :], ALU.mult), reads=[r_sa, r_cs], writes=[r_S1])
                k.op("dve", lambda e: e.tensor_tensor(S1[:, :], S1[:, :], m1[:, :], ALU.add), reads=[r_S1, r_m1_], writes=[r_S1])
                yield
                r8b = R8[:, g:g + 1].to_broadcast([128, NT])
                osl = slice(NPRE - 1, NT)
                k.op("dve", lambda e: e.tensor_tensor_scan(G1[:, :], r8b, S1[:, :], 0.0, ALU.mult, ALU.add),
                     reads=[r_R8, r_S1], writes=[r_G1])
                (r_g2lo, r_g2hi) = r_G2h[i2]
                k.dma("sp", G2[0:64, osl], G1[64:128, osl], reads=[r_G1], writes=[r_g2lo])
                k.dma("sp", G2[64:128, osl], G1[0:64, osl], reads=[r_G1], writes=[r_g2hi])
                yield
                (h1, r_h1), (h2_, r_h2) = W["h1"][i2], W["h2"][i2]
                k.op("dve", lambda e: e.tensor_tensor(h1[:, osl], G1[:, osl], cs[:, osl], ALU.mult), reads=[r_G1, r_cs], writes=[r_h1])
                k.op("dve", lambda e: e.tensor_tensor(h2_[:, osl], G2[:, osl], m2[:, osl], ALU.mult), reads=[r_g2lo, r_g2hi, r_m2_], writes=[r_h2])
                (hb, r_hb) = Hb[i2]
                k.op("dve", lambda e: e.tensor_tensor(hb[:, :], h1[:, NPRE - 1:NT - 1], h2_[:, NPRE - 1:NT - 1], ALU.add),
                     reads=[r_h1, r_h2], writes=[r_hb])
                k.op("pool", lambda e: e.tensor_tensor(Hend[:, g:g + 1], h1[:, NT - 1:NT], h2_[:, NT - 1:NT], ALU.add),
                     reads=[r_h1, r_h2], writes=[r_Hend])
                (py, r_py) = pY[g % 2]
                k.op("pe", lambda e: e.matmul(py[:, :], Mbf[:, g, :], USJ[:, g, NPRE:NT], start=True, stop=False),
                     reads=[r_Mbf] + r_USJl, writes=[r_py])
                k.op("pe", lambda e: e.matmul(py[:, :], Qbf[:, g, :], hb[:, :], start=False, stop=True),
                     reads=[r_Qbf, r_hb], writes=[r_py])
                k.op("act", lambda e: e.copy(YSC[:, g, :], py[:, :]), reads=[r_py], writes=[r_YSCb[g // 8]])
                k.op("pe", lambda e: e.matmul(pysv[:, g, :], Mbf[:, g, 64:128], USJs[:, g, :], start=True, stop=False),
                     reads=[r_Mbf, r_USJs] + r_USJsl, writes=[r_pys])
                k.op("pe", lambda e: e.matmul(pysv[:, g, :], Qbf[:, g, 0:64], h0b[:, g, :], start=False, stop=True),
                     reads=[r_Qbf, r_h0b], writes=[r_pys])
                k.op("pe", lambda e: e.matmul(pssv[:, g, :], Pbf[:, g, :], USJs[:, g, :], start=True, stop=True),
                     reads=[r_Pbf, r_USJs] + r_USJsl, writes=[r_pss])
                if g % 8 == 7:
                    gb0 = g - 7
                    for s_ in range(8):
                        k.dma("sp", Yd[s_][gb0 * 16:(gb0 + 8) * 16, :].rearrange("(g c) n -> c g n", c=16),
                              YSC[16 * s_:16 * s_ + 16, gb0:gb0 + 8, :], reads=[r_YSCb[g // 8]], writes=[r_Ydb[g // 8][s_]])
            sg = [ssm_group(g) for g in range(32)]

            def seg(i):
                try:
                    next(sg[i])
                except StopIteration:
                    pass

            seg(0); seg(0)
            for g in range(32):
                if g + 1 < 32:
                    seg(g + 1)
                seg(g)
                if g >= 1:
                    seg(g - 1); seg(g - 1)
                seg(g)
                if g + 1 < 32:
                    seg(g + 1)
            seg(31); seg(31)
            k.dma("sp", hend_o[:, :], Hend[:, :], reads=[r_Hend])
            YSs, r_YSs = SB(stC, "YSs", [64, 32, 16], BF16)
            k.op("act", lambda e: e.copy(YSs[:, :, :], pysv), reads=[r_pys], writes=[r_YSs])
            hn1, r_hn1 = SB(stC, "hn1", [128, 32, 16]); hn2, r_hn2 = SB(stC, "hn2", [128, 32, 16])
            s5 = [128, 32, 16]
            k.op("dve", lambda e: e.tensor_tensor(hn1[:, :, :], h0s[:, :, :], bc(E4r[:, :], 2, s5), ALU.mult), reads=[r_h0s, r_E4], writes=[r_hn1])
            k.op("pool", lambda e: e.tensor_tensor(hn2[:, :, :], h0d[:, :, :], bc(E4i[:, :], 2, s5), ALU.mult), reads=[r_h0d, r_E4], writes=[r_hn2])
            k.op("dve", lambda e: e.tensor_tensor(hn1[:, :, :], hn1[:, :, :], hn2[:, :, :], ALU.add), reads=[r_hn1, r_hn2], writes=[r_hn1])
            k.op("dve", lambda e: e.tensor_tensor(hn1[:, :, :], hn1[:, :, :], pssv, ALU.add), reads=[r_hn1, r_pss], writes=[r_hn1])
            k.dma("sp", hs_new[:, :, :], hn1[:, :, :], reads=[r_hn1])
            r_Yd = Res(); r_Ysd = Res()
            r_Ysds = [Res() for _ in range(4)]
            for t_ in range(4):
                k.dma("sp", Ysd[t_].rearrange("(g c) b -> c g b", c=16), YSs[16 * t_:16 * t_ + 16, :, :], reads=[r_YSs], writes=[r_Ysds[t_]])
            k.wait_all("sp", [r_Hend, r_hn1])
            k.barrier()
        stAC.close()

        stC2 = ExitStack()
        with stC2:
            YT, r_YT = SB(stC2, "YT", [128, 4, NCOL], BF16)
            UTm, r_UTm = SB(stC2, "UTm", [128, 4, NCOL], BF16)
            r_YTl = [Res() for _ in range(8)]; r_UTml = [Res() for _ in range(8)]
            for ch in range(4):
                k.dma("sp", YT[:, ch, 0:2176].rearrange("p (s n) -> p s n", s=8),
                      Yd[:, ch * 128:(ch + 1) * 128, :].rearrange("s p n -> p s n"), reads=r_Ydb[ch], writes=[r_YTl[ch]])
                k.dma("sp", YT[:, ch, 2176:2240].rearrange("p (t b) -> p t b", t=4),
                      Ysd[:, ch * 128:(ch + 1) * 128, :].rearrange("t p b -> p t b"), reads=r_Ysds, writes=[r_YTl[4 + ch]])
            for ch in range(4):
                k.dma("sp", UTm[:, ch, 0:2176].rearrange("p (s n) -> p s n", s=8), Ud[ch][:, :, NPRE:NT],
                      reads=[r_Udl[ch]], writes=[r_UTml[ch]])
                k.dma("sp", UTm[:, ch, 2176:2240].rearrange("p (t b) -> p t b", t=4), Usd[ch], reads=[r_Usdl[ch]], writes=[r_UTml[4 + ch]])
            dvec, r_dvec = ld(stC2, "dvec", dvec_d, [128, 4])
            wglu, r_wglu = SB(stC2, "wglu", [128, 4, 512], BF16)
            k.dma("pool", wglu[:, :, :], w_glu_d.rearrange("(c p) n -> p c n", p=128), writes=[r_wglu])
            z, r_z = SB(stC2, "z", [128, 4, NCOL])
            zb, r_zb = SB(stC2, "zb", [128, 4, NCOL], BF16)
            HC = NCOL // 2
            yfl = [SB(stC2, "yf%d" % i, [128, HC]) for i in range(3)]
            ttl = [SB(stC2, "tt%d" % i, [128, HC]) for i in range(3)]

            def gelu_unit(u):
                ch, hh = u // 2, u % 2
                cs_ = slice(hh * HC, (hh + 1) * HC)
                (yf, r_yf), (tt, r_tt) = yfl[u % 3], ttl[u % 3]
                k.op("dve", lambda e: e.scalar_tensor_tensor(yf[:, :], UTm[:, ch, cs_], dvec[:, ch:ch + 1], YT[:, ch, cs_], ALU.mult, ALU.add),
                     reads=[r_UTml[ch], r_UTml[4 + ch], r_dvec, r_YTl[ch], r_YTl[4 + ch]], writes=[r_yf])
                k.op("act", lambda e: e.activation(tt[:, :], yf[:, :], AF.Square), reads=[r_yf], writes=[r_tt])
                yield
                k.op("dve", lambda e: e.tensor_scalar(tt[:, :], tt[:, :], 0.044715, 1.0, ALU.mult, ALU.add), reads=[r_tt], writes=[r_tt])
                k.op("dve", lambda e: e.tensor_tensor(tt[:, :], tt[:, :], yf[:, :], ALU.mult), reads=[r_tt, r_yf], writes=[r_tt])
                k.op("act", lambda e: e.activation(tt[:, :], tt[:, :], AF.Sigmoid, scale=2.0 * math.sqrt(2.0 / math.pi)),
                     reads=[r_tt], writes=[r_tt])
                yield
                k.op("dve", lambda e: e.tensor_tensor(z[:, ch, cs_], yf[:, :], tt[:, :], ALU.mult), reads=[r_yf, r_tt], writes=[r_z])
                k.op("act", lambda e: e.copy(zb[:, ch, cs_], z[:, ch, cs_]), reads=[r_z], writes=[r_zb])

            run_staged([gelu_unit(u) for u in range(8)], 3, oldest_first=True)
            pg = [PS(stC2, "pg%d" % i, [128, 512]) for i in range(3)]
            pss2, r_pss2 = PS(stC2, "pss2", [128, 512])
            sg = [SB(stC2, "sg%d" % i, [128, 512]) for i in range(2)]
            sq = [SB(stC2, "sq%d" % i, [128, 512], BF16) for i in range(2)]
            rstd, r_rstd = SB(stC2, "rstd", [128, NCOL])
            cblocks = [(c0, min(512, NCOL - c0)) for c0 in range(0, NCOL, 512)]
            def gate_unit(cnt, c0, cw, ec):
                csl = slice(c0, c0 + cw)
                (pg_, r_pg) = pg[cnt % 3]; (sg_, r_sg) = sg[cnt % 2]; (sq_, r_sq) = sq[cnt % 2]
                for cc in range(4):
                    k.op("pe", lambda e: e.matmul(pg_[:, 0:cw], wglu[:, cc, ec * 128:(ec + 1) * 128], zb[:, cc, csl],
                                                  start=(cc == 0), stop=(cc == 3)),
                         reads=[r_wglu, r_zb], writes=[r_pg], sig=SG("C2_g", cc == 3), chain=CH("C2_g", cc > 0))
                yield
                k.op("act", lambda e: e.activation(sg_[:, 0:cw], pg_[:, 0:cw], AF.Sigmoid), reads=[r_pg], writes=[r_sg])
                yield
                k.op("dve", lambda e: e.tensor_tensor(z[:, ec, csl], z[:, ec, csl], sg_[:, 0:cw], ALU.mult),
                     reads=[r_z, r_sg, r_zb], writes=[r_z])
                k.op("act", lambda e: e.activation(sq_[:, 0:cw], z[:, ec, csl], AF.Square),
                     reads=[r_z], writes=[r_sq])
                k.op("pe", lambda e: e.matmul(pss2[:, 0:cw], onesb[:, :], sq_[:, 0:cw], start=(ec == 0), stop=(ec == 3)),
                     reads=[r_onesb, r_sq], writes=[r_pss2])
                if ec == 3:
                    k.op("act", lambda e: e.activation(rstd[:, csl], pss2[:, 0:cw], AF.Sqrt, bias=EPS, scale=1.0 / 512),
                         reads=[r_pss2], writes=[r_rstd])

            units = []
            for (c0_, cw_) in cblocks:
                for ec in range(4):
                    units.append(gate_unit(len(units), c0_, cw_, ec))
            def useg(i):
                try:
                    next(units[i])
                except StopIteration:
                    pass

            useg(0); useg(0)
            for i in range(len(units)):
                if i + 1 < len(units):
                    useg(i + 1)
                useg(i)
                if i + 1 < len(units):
                    useg(i + 1)
            k.op("dve", lambda e: e.reciprocal(rstd[:, :], rstd[:, :]), reads=[r_rstd], writes=[r_rstd])
            for ec in range(4):
                k.op("dve",
                     lambda e: e.tensor_tensor(catT[:, 4 + ec, 0:2176].rearrange("p (n s) -> p s n", s=8),
                                               z[:, ec, 0:2176].rearrange("p (s n) -> p s n", s=8),
                                               rstd[:, 0:2176].rearrange("p (s n) -> p s n", s=8), ALU.mult),
                     reads=[r_z, r_rstd], writes=[r_catT])
                k.op("dve",
                     lambda e: e.tensor_tensor(catT[:, 4 + ec, 2176:2240].rearrange("p (b t) -> p t b", t=4),
                                               z[:, ec, 2176:2240].rearrange("p (t b) -> p t b", t=4),
                                               rstd[:, 2176:2240].rearrange("p (t b) -> p t b", t=4), ALU.mult),
                     reads=[r_z, r_rstd], writes=[r_catT])
            k.barrier()

        stD = ExitStack()
        with stD:
            wup, r_wup = SB(stD, "wup", [128, 8, 4096], BF16)
            wdn, r_wdn = SB(stD, "wdn", [128, 32, 1024], BF16)
            gmlp, r_gmlp = ld(stD, "gmlp", gmlp_d, [128, 8])
            r_cat = [Res() for _ in range(18)]
            r_y = [Res() for _ in range(18)]
            dtiles = [(i, 128) for i in range(1, 17)] + [(17, 64)]

            def yrows(ti):
                return y_main[(ti - 1) * 128: ti * 128, :] if ti < 17 else y_s[:, :]

            stD1 = ExitStack()
            with stD1:
                wout, r_wout = SB(stD1, "wout", [128, 8, 1024], BF16)
                k.dma("pool", wout[:, :, :], w_out_d.rearrange("(c p) n -> p c n", p=128), writes=[r_wout])
                gcat, r_gcat = ld(stD1, "gcat", gcat_d, [128, 8])
                gpost, r_gpost = ld(stD1, "gpost", gpost_d, [128, 1024], BF16, q="pool")
                r_wupc = [Res() for _ in range(8)]
                for c in range(8):
                    k.dma("pool", wup[:, c, :], w_up_d[c * 128:(c + 1) * 128, :], writes=[r_wupc[c]])
                r_wdnc = [Res() for _ in range(4)]
                for c in range(4):
                    k.dma("pool", wdn[:, c * 8:(c + 1) * 8, :], w_down_d[c * 1024:(c + 1) * 1024, :].rearrange("(c p) n -> p c n", p=128),
                          writes=[r_wdnc[c]])
                for c in range(8):
                    k.op("dve", lambda e: e.tensor_scalar(wout[:, c, :], wout[:, c, :], gcat[:, c:c + 1], None, ALU.mult),
                         reads=[r_gcat, r_wout], writes=[r_wout])
                xr = [SB(stD1, "xr%d" % i, [128, 1024]) for i in range(2)]
                h1t = [SB(stD1, "h1t%d" % i, [128, 1024]) for i in range(2)]
                xn2 = [SB(stD1, "xn2%d" % i, [128, 1024], BF16) for i in range(2)]
                sdl = [SB(stD1, "sd%d" % i, [128, 1]) for i in range(2)]
                sd2l = [SB(stD1, "sd2%d" % i, [128, 1]) for i in range(2)]
                junkd, r_junkd = SB(stD1, "junkd", [128, 1024], BF16)
                pmixl = [PS(stD1, "pmix%d" % i, [128, 2, 512]) for i in range(2)]
                ptpl = [PS(stD1, "ptp%d" % i, [128, 8, 128], BF16) for i in range(2)]
                def d1_body(it, ti, np_):
                    PSL = slice(0, np_)
                    b2 = it % 2
                    (xr_, r_xr), (h1_, r_h1), (xn_, r_xn), (sd, r_sd), (sd2, r_sd2) = xr[b2], h1t[b2], xn2[b2], sdl[b2], sd2l[b2]
                    (pmix, r_pmix), (ptp, r_ptp) = pmixl[b2], ptpl[b2]
                    col0 = ti * 128
                    k.dma("sp", xr_[PSL, :], (xm[ti * 128: ti * 128 + 128, :] if ti < 17 else xs[:, :]), writes=[r_xr])
                    for hf in range(2):
                        for e_ in range(8):
                            k.op("pe", lambda e: e.matmul(pmix[PSL, hf, :], catT[:, e_, col0:col0 + np_], wout[:, e_, hf * 512:(hf + 1) * 512],
                                                          start=(e_ == 0), stop=(e_ == 7)),
                                 reads=[r_cat[ti], r_wout], writes=[r_pmix], sig=SG("D_o", e_ == 7 and hf == 1), chain=CH("D_o", e_ > 0))
                    yield
                    pmv = pmix[PSL, :, :].rearrange("p a n -> p (a n)")
                    k.op("act", lambda e: e.activation(junkd[PSL, :], pmv, AF.Square, accum_out=sd[PSL, :]),
                         reads=[r_pmix], writes=[r_junkd, r_sd])
                    k.op("act", lambda e: e.activation(sd[PSL, :], sd[PSL, :], AF.Sqrt, bias=EPS, scale=1.0 / 1024), reads=[r_sd], writes=[r_sd])
                    k.op("dve", lambda e: e.reciprocal(sd[PSL, :], sd[PSL, :]), reads=[r_sd], writes=[r_sd])
                    k.op("dve", lambda e: e.scalar_tensor_tensor(h1_[PSL, :], pmv, sd[PSL, :], gpost[PSL, :], ALU.mult, ALU.mult),
                         reads=[r_pmix, r_sd, r_gpost], writes=[r_h1])
                    k.op("dve", lambda e: e.tensor_tensor(h1_[PSL, :], h1_[PSL, :], xr_[PSL, :], ALU.add), reads=[r_h1, r_xr], writes=[r_h1])
                    k.dma("sp", yrows(ti), h1_[PSL, :], reads=[r_h1], writes=[r_y[ti]])
                    yield
                    k.op("act", lambda e: e.activation(junkd[PSL, :], h1_[PSL, :], AF.Square, accum_out=sd2[PSL, :]),
                         reads=[r_h1], writes=[r_junkd, r_sd2])
                    k.op("act", lambda e: e.activation(sd2[PSL, :], sd2[PSL, :], AF.Sqrt, bias=EPS, scale=1.0 / 1024), reads=[r_sd2], writes=[r_sd2])
                    k.op("dve", lambda e: e.reciprocal(sd2[PSL, :], sd2[PSL, :]), reads=[r_sd2], writes=[r_sd2])
                    k.op("act", lambda e: e.activation(xn_[PSL, :], h1_[PSL, :], AF.Copy, scale=sd2[PSL, :]), reads=[r_h1, r_sd2], writes=[r_xn])
                    yield
                    for c in range(8):
                        k.op("pe", lambda e: e.transpose(ptp[:, c, PSL], xn_[PSL, c * 128:(c + 1) * 128], ident[PSL, PSL]),
                             reads=[r_xn, r_ident], writes=[r_ptp], sig=SG("D_tp", c == 7))
                    k.op("dve", lambda e: e.tensor_copy(catT[:, :, col0:col0 + np_], ptp[:, :, PSL]), reads=[r_ptp], writes=[r_cat[ti]])
                run_staged([d1_body(it, ti, np_) for it, (ti, np_) in enumerate(dtiles)], 4)
                k.barrier()

            stD2 = ExitStack()
            with stD2:
                for c in range(8):
                    k.op("dve", lambda e: e.tensor_scalar(wup[:, c, :], wup[:, c, :], gmlp[:, c:c + 1], None, ALU.mult),
                         reads=[r_gmlp], writes=[r_wupc[c]])
                gmpost, r_gmpost = ld(stD2, "gmpost", gmpost_d, [128, 1024], BF16, q="pool")
                h1r = [SB(stD2, "h1r%d" % i, [128, 1024]) for i in range(2)]
                tmpl = [SB(stD2, "tmp%d" % i, [128, 1024]) for i in range(2)]
                junk2, r_junk2 = SB(stD2, "junk2", [128, 1024], BF16)
                aTw = [SB(stD2, "aTw%d" % i, [128, 2, 256], BF16) for i in range(3)]
                rl = [SB(stD2, "rl%d" % i, [128, 2, 256], BF16) for i in range(2)]
                sd3l = [SB(stD2, "sd3%d" % i, [128, 1]) for i in range(2)]
                pdn = [PS(stD2, "pdn%d" % i, [128, 2, 512]) for i in range(2)]
                pup = [PS(stD2, "pup%d" % i, [128, 2, 256]) for i in range(2)]
                groups = [[(2 * g + 1, 128), (2 * g + 2, 128)] for g in range(8)]
                groups.append([(17, 64)])
                NF2 = 16
                for gi, tiles in enumerate(groups):
                    ncols = sum(np_ for _, np_ in tiles)
                    c0 = tiles[0][0] * 128
                    CS = slice(c0, c0 + ncols)
                    rcs = [r_cat[ti] for ti, _ in tiles]
                    for li, (ti, np_) in enumerate(tiles):
                        (h1_, r_h1) = h1r[li]
                        k.dma("sp", h1_[0:np_, :], yrows(ti), reads=[r_y[ti]], writes=[r_h1])

                    def up(f2):
                        (pu_, r_pu_) = pup[f2 % 2]; (rl_, r_rl) = rl[f2 % 2]; (aw, r_aw) = aTw[f2 % 3]
                        for fi in range(2):
                            fc = f2 * 2 + fi
                            for c in range(8):
                                k.op("pe", lambda e: e.matmul(pu_[:, fi, 0:ncols], wup[:, c, fc * 128:(fc + 1) * 128], catT[:, c, CS],
                                                              start=(c == 0), stop=(c == 7)),
                                     reads=[r_wupc[c]] + rcs, writes=[r_pu_], sig=SG("D_up", c == 7 and fi == 1), chain=CH("D_up", c > 0))
                        k.op("act", lambda e: e.activation(rl_[:, :, 0:ncols], pu_[:, :, 0:ncols], AF.Relu), reads=[r_pu_], writes=[r_rl])
                        k.op("dve", lambda e: e.tensor_tensor(aw[:, :, 0:ncols], rl_[:, :, 0:ncols], rl_[:, :, 0:ncols], ALU.mult),
                             reads=[r_rl], writes=[r_aw])

                    def down(f2):
                        (aw, r_aw) = aTw[f2 % 3]
                        to = 0
                        for li, (ti, np_) in enumerate(tiles):
                            (pd_, r_pd) = pdn[li]
                            for hf in range(2):
                                for fi in range(2):
                                    fc = f2 * 2 + fi
                                    k.op("pe", lambda e: e.matmul(pd_[0:np_, hf, :], aw[:, fi, to:to + np_], wdn[:, fc, hf * 512:(hf + 1) * 512],
                                                                  start=(fc == 0), stop=(fc == 2 * NF2 - 1)),
                                         reads=[r_aw, r_wdnc[fc // 8]], writes=[r_pd], sig=SG("D_dn", hf == 1 and fi == 1), chain=CH("D_dn", fc > 0))
                            to += np_

                    up(0)
                    for f2 in range(NF2):
                        if f2 + 1 < NF2:
                            up(f2 + 1)
                        down(f2)
                    for li, (ti, np_) in enumerate(tiles):
                        PSL = slice(0, np_)
                        (pd_, r_pd) = pdn[li]; (h1_, r_h1) = h1r[li]; (tmp, r_tmp) = tmpl[li]; (sd3, r_sd3) = sd3l[li]
                        pdv = pd_[PSL, :, :].rearrange("p a n -> p (a n)")
                        k.op("act", lambda e: e.activation(junk2[PSL, :], pdv, AF.Square, accum_out=sd3[PSL, :]),
                             reads=[r_pd], writes=[r_junk2, r_sd3])
                        k.op("act", lambda e: e.activation(sd3[PSL, :], sd3[PSL, :], AF.Sqrt, bias=EPS, scale=1.0 / 1024), reads=[r_sd3], writes=[r_sd3])
                        k.op("dve", lambda e: e.reciprocal(sd3[PSL, :], sd3[PSL, :]), reads=[r_sd3], writes=[r_sd3])
                        k.op("dve", lambda e: e.scalar_tensor_tensor(tmp[PSL, :], pdv, sd3[PSL, :], gmpost[PSL, :], ALU.mult, ALU.mult),
                             reads=[r_pd, r_sd3, r_gmpost], writes=[r_tmp])
                        k.op("pool", lambda e: e.tensor_tensor(tmp[PSL, :], tmp[PSL, :], h1_[PSL, :], ALU.add), reads=[r_tmp, r_h1], writes=[r_tmp])
                        k.dma("sp", yrows(ti), tmp[PSL, :], reads=[r_tmp], writes=[r_y[ti]])
                k.wait_all("sp", r_y)
                k.barrier()
        print("n_inst", k.n_inst)
    return nc


_NC_CACHE = {}


def _host_inputs(inp):
    f = np.float32
    x_prompt = np.asarray(inp["x_prompt"], f); x_sample = np.asarray(inp["x_sample"], f)
    meta = np.asarray(inp["meta_tokens"], f)
    w_in = np.asarray(inp["w_in"], f)[0]
    w_in_ext = np.concatenate([w_in[:, 0:512], w_in[:, 512:576], w_in[:, 512:576], w_in[:, 576:640], w_in[:, 576:640],
                               w_in[:, 640:768], w_in[:, 768:1280]], axis=1)

    def pc(v):
        return np.ascontiguousarray(np.asarray(v, f).reshape(8, 128).T)

    gcat = np.concatenate([np.asarray(inp["norm_att_out"], f)[0], np.asarray(inp["norm_ssm_out"], f)[0]])
    a_re = np.asarray(inp["ssm_a_re"], f)[0]; a_im = np.asarray(inp["ssm_a_im"], f)[0]
    dup = lambda a: np.ascontiguousarray(np.concatenate([a, a], axis=0))
    b_re = np.asarray(inp["ssm_b_re"], f)[0]; b_im = np.asarray(inp["ssm_b_im"], f)[0]
    c_re = np.asarray(inp["ssm_c_re"], f)[0]; c_im = np.asarray(inp["ssm_c_im"], f)[0]
    half = 8
    invf = (500000.0 ** (-np.arange(half, dtype=np.float64) / half) / (2 * np.pi)).astype(f)
    s_i = np.arange(128)[:, None]; q_i = np.arange(128)[None, :]
    maskp = (s_i > q_i).astype(f); maskc = (s_i <= q_i).astype(f)
    ks = np.arange(64)[:, None]; qs = np.arange(64)[None, :]
    masks = np.zeros((128, 64), f)
    masks[:64] = ((ks // 4 == qs // 4) & (ks % 4 <= qs % 4)).astype(f)
    sj = np.arange(128)[:, None] // 16; tc = np.arange(128)[None, :] // 16
    maskM = (tc >= sj).astype(f)
    common = {
        "invf": np.broadcast_to(invf, (128, 8)).copy(),
        "maskp": maskp, "maskc": maskc, "masks": masks, "maskM": maskM,
        "iota_n": np.broadcast_to(np.arange(NT, dtype=f), (128, NT)).copy(),
        "kvals": np.broadcast_to(np.arange(-7, 9, dtype=f), (128, 16)).copy(),
        "kvals_r": np.broadcast_to(np.arange(8, -8, -1, dtype=f), (128, 16)).copy(),
        "w_in": np.ascontiguousarray(w_in_ext), "w_out": np.asarray(inp["w_out"], f)[0],
        "w_up": np.asarray(inp["w_up"], f)[0], "w_down": np.asarray(inp["w_down"], f)[0],
        "w_glu": np.asarray(inp["w_glu"], f)[0],
        "gpre": pc(inp["norm_mix_pre"][0]), "gcat": pc(gcat), "gmlp": pc(inp["norm_mlp_pre"][0]),
        "gpost": np.broadcast_to(np.asarray(inp["norm_mix_post"], f)[0], (128, 1024)).copy(),
        "gmpost": np.broadcast_to(np.asarray(inp["norm_mlp_post"], f)[0], (128, 1024)).copy(),
        "sinks": np.broadcast_to(np.asarray(inp["attn_sinks"], f)[0], (128, 8)).copy(),
        "dvec": np.ascontiguousarray(np.asarray(inp["ssm_d"], f)[0].reshape(4, 128).T),
        "art": dup(a_re.T), "ait": dup(a_im.T),
        "ldt": np.broadcast_to(np.asarray(inp["ssm_log_dt"], f)[0], (128, 32)).copy(),
        "btr": dup(b_re.transpose(1, 0, 2)), "bti": dup(b_im.transpose(1, 0, 2)),
        "ctr": dup(c_re.transpose(2, 0, 1)), "cti": dup(c_im.transpose(2, 0, 1)),
    }
    ck = np.asarray(inp["cache_k_win"], f)[0].reshape(128, 128, 128)
    cv = np.asarray(inp["cache_v_win"], f)[0].reshape(128, 128, 128)
    sre = np.asarray(inp["state_ssm_re"], f)[0]; sim = np.asarray(inp["state_ssm_im"], f)[0]
    maps = []
    for c in range(8):
        b, hf = c // 2, c % 2
        m = dict(common)
        xm = np.zeros((2176, 1024), f); xp = np.zeros((2048, 1024), f)
        pos = np.zeros((128, 18), f)
        p_i = np.arange(128)
        if hf == 0:
            xm[112:128] = meta; xm[128:] = x_prompt[b, 0:2048]
            for i in range(17):
                pos[:, i] = np.maximum(i * 128 + p_i - 112, 0)
            mask1 = maskp * (s_i >= 112)
        else:
            xm[:] = x_prompt[b, 1920:4096]
            xp[112:128] = meta; xp[128:] = x_prompt[b, 0:1920]
            for i in range(17):
                pos[:, i] = 1936 + i * 128 + p_i
            mask1 = maskp
        pos[:, 17] = 8192 + (p_i % 4)
        m["xm"] = xm; m["xp"] = xp
        m["xs"] = np.ascontiguousarray(x_sample[16 * c:16 * c + 16].reshape(64, 1024))
        m["pos"] = pos; m["mask1"] = np.ascontiguousarray(mask1.astype(f))
        ckc = ck[16 * c:16 * c + 16]; cvc = cv[16 * c:16 * c + 16]
        m["ck"] = np.ascontiguousarray(ckc); m["cv"] = np.ascontiguousarray(cvc)
        m["ckd"] = np.ascontiguousarray(np.concatenate([ckc[:, :, 0:64], ckc[:, :, 0:64], ckc[:, :, 64:128], ckc[:, :, 64:128]], axis=2))
        r_ = sre[16 * c:16 * c + 16].transpose(2, 1, 0); i_ = sim[16 * c:16 * c + 16].transpose(2, 1, 0)
        m["h0s"] = np.ascontiguousarray(np.concatenate([r_, i_], axis=0))
        m["h0d"] = np.ascontiguousarray(np.concatenate([i_, r_], axis=0))
        maps.append(m)
    return maps


def kernel(**inputs):
    if "nc" not in _NC_CACHE:
        _NC_CACHE["nc"] = build_program()
    nc = _NC_CACHE["nc"]
    maps = _host_inputs(inputs)
    res = run_bass_kernel_spmd(nc, maps, core_ids=list(range(8)))
    R = res.results
    f = np.float32
    y_prompt = np.zeros((4, 4096, 1024), f)
    kwp = np.zeros((1, 4, 128, 2, 64), f); vwp = np.zeros((1, 4, 128, 2, 64), f)
    srp = np.zeros((1, 4, 32, 64), f); sip = np.zeros((1, 4, 32, 64), f)
    y_sample = np.zeros((128, 4, 1024), f)
    kws = np.zeros((1, 128, 128, 2, 64), f); vws = np.zeros((1, 128, 128, 2, 64), f)
    srs = np.zeros((1, 128, 32, 64), f); sis = np.zeros((1, 128, 32, 64), f)
    for c in range(8):
        b, hf = c // 2, c % 2
        r = R[c]
        ym = np.asarray(r["y_main"], f)
        if hf == 0:
            y_prompt[b, 0:2048] = ym
        else:
            y_prompt[b, 2048:4096] = ym
            kwp[0, b] = np.asarray(r["kwin"], f).reshape(128, 2, 64)
            vwp[0, b] = np.asarray(r["vwin"], f).reshape(128, 2, 64)
            he = np.asarray(r["hend"], f)
            srp[0, b] = he[0:64].T; sip[0, b] = he[64:128].T
        sl = slice(16 * c, 16 * c + 16)
        y_sample[sl] = np.asarray(r["y_s"], f).reshape(16, 4, 1024)
        kws[0, sl, 0:124] = np.asarray(r["kc_s"], f).reshape(16, 124, 2, 64)
        kws[0, sl, 124:128] = np.asarray(r["knew"], f).reshape(16, 4, 2, 64)
        vws[0, sl, 0:124] = np.asarray(r["vc_s"], f).reshape(16, 124, 2, 64)
        vws[0, sl, 124:128] = np.asarray(r["vnew"], f).reshape(16, 4, 2, 64)
        hn = np.asarray(r["hs_new"], f)
        srs[0, sl] = hn[0:64].transpose(2, 1, 0); sis[0, sl] = hn[64:128].transpose(2, 1, 0)
    return (y_prompt, y_sample, kwp, vwp, srp, sip, kws, vws, srs, sis)
```

```python
import math
from contextlib import ExitStack
import numpy as np
import concourse.bass as bass
import concourse.mybir as mybir
from concourse.bass_utils import run_bass_kernel_spmd

F32 = mybir.dt.float32
BF16 = mybir.dt.bfloat16
I32 = mybir.dt.int32
ALU = mybir.AluOpType
AF = mybir.ActivationFunctionType

NPRE = 256
NMAIN = 272
NT = NPRE + NMAIN
NTILE_P = 16
NTILE_M = 17
NCOL = 2240
TWO_PI_S = 6.2831
EPS = 1e-6


class Res:
    __slots__ = ("w", "r")

    def __init__(self):
        self.w = None
        self.r = []


class K:
    NDMA = 6

    def __init__(self, nc, stack):
        self.nc = nc
        self.eng = {"pe": nc.tensor, "dve": nc.vector, "act": nc.scalar,
                    "pool": nc.gpsimd, "sp": nc.sync}
        self.sems = {}
        self.cnt = {}
        self.seen = {e: {} for e in self.eng}
        for e in self.eng:
            self.sems[e] = stack.enter_context(nc.semaphore("s_" + e))
            self.cnt[e] = 0
        self.dq = {}
        for q, nq in (("sp", 24), ("pool", 12), ("act", 2)):
            lst = []
            for i in range(nq):
                key = "d_%s%d" % (q, i)
                self.sems[key] = stack.enter_context(nc.semaphore(key))
                self.cnt[key] = 0
                lst.append(key)
            self.dq[q] = [lst, 0]
        self.n_inst = 0

    def _wait(self, e, dep):
        if dep is None:
            return
        key, val = dep
        if key == "pe" and e == "pe" and (self.pe_chain or val > self.cnt["pe"]):
            return
        if self.seen[e].get(key, 0) >= val:
            return
        self.eng[e].wait_ge(self.sems[key], val)
        self.seen[e][key] = val

    def _deps(self, e, reads, writes):
        for r in reads:
            self._wait(e, r.w)
        for w in writes:
            self._wait(e, w.w)
            for d in w.r:
                self._wait(e, d)

    def _commit(self, tok, reads, writes):
        for r in reads:
            r.r.append(tok)
            if len(r.r) > 24:
                best = {}
                for kk, v in r.r:
                    if best.get(kk, 0) < v:
                        best[kk] = v
                r.r = list(best.items())
        for w in writes:
            w.w = tok
            w.r = []

    pe_chain = False

    def op(self, e, fn, reads=(), writes=(), sig=True, chain=False):
        self.pe_chain = chain and e == "pe"
        self._deps(e, reads, writes)
        self.pe_chain = False
        ins = fn(self.eng[e])
        if sig:
            self.cnt[e] += 1
            ins.then_inc(self.sems[e], 1)
            tok = (e, self.cnt[e])
        else:
            assert e == "pe"
            tok = (e, self.cnt[e] + 1)
        self._commit(tok, reads, writes)
        self.n_inst += 1
        return ins

    def dma(self, q, out, in_, reads=(), writes=(), **kw):
        self._deps(q, reads, writes)
        lst, i = self.dq[q]
        key = lst[i % len(lst)]
        self.dq[q][1] = i + 1
        if self.cnt[key] > 0:
            self._wait(q, (key, self.cnt[key]))
        ins = self.eng[q].dma_start(out=out, in_=in_, **kw)
        self.cnt[key] += 16
        ins.then_inc(self.sems[key], 16)
        self._commit((key, self.cnt[key]), reads, writes)
        self.n_inst += 1
        return ins

    def wait_all(self, e, ress):
        for r in ress:
            self._wait(e, r.w)
            for d in r.r:
                self._wait(e, d)

    def barrier(self):
        for e in self.eng:
            for key in self.sems:
                if self.cnt[key] > 0 and not (key == "pe" and e == "pe"):
                    self._wait(e, (key, self.cnt[key]))


import os
_SITES = set(os.environ.get("KSITES", "ALL").split(","))


def SG(site, cond):
    return bool(cond) if ("ALL" in _SITES or site in _SITES) else True


def CH(site, cond):
    return bool(cond) if ("ALL" in _SITES or site in _SITES) else False


def run_staged(gens, nstages, oldest_first=False, order=None):
    gens = list(gens)
    n = len(gens)
    for it in range(n + nstages - 1):
        for s in (order if order is not None else (range(nstages - 1, -1, -1) if oldest_first else range(nstages))):
            i = it - s
            if 0 <= i < n:
                try:
                    next(gens[i])
                except StopIteration:
                    pass
    for g_ in gens:
        for _ in g_:
            pass


def run_pipelined(gens, depth=1):
    gens = list(gens)
    n = len(gens)
    alive = [True] * n

    def step(i):
        if alive[i]:
            try:
                next(gens[i])
            except StopIteration:
                alive[i] = False

    for i in range(min(depth, n)):
        step(i)
    for i in range(n):
        if i + depth < n:
            step(i + depth)
        while alive[i]:
            step(i)


def bc(ap, axis, shape):
    return ap.unsqueeze(axis).to_broadcast(list(shape))


def build_program(debug=False):
    nc = bass.Bass("TRN2", target_bir_lowering=False)

    def DI(name, shape, dt=F32):
        return nc.dram_tensor(name, list(shape), dt, kind="ExternalInput").ap()

    def DO(name, shape, dt=F32):
        return nc.dram_tensor(name, list(shape), dt, kind="ExternalOutput").ap()

    def DS(name, shape, dt=F32):
        return nc.dram_tensor(name, list(shape), dt, kind="Internal").ap()

    xm = DI("xm", [2176, 1024]); xp = DI("xp", [2048, 1024]); xs = DI("xs", [64, 1024])
    pos_d = DI("pos", [128, 18]); invf_d = DI("invf", [128, 8])
    mask1_d = DI("mask1", [128, 128]); maskp_d = DI("maskp", [128, 128]); maskc_d = DI("maskc", [128, 128])
    masks_d = DI("masks", [128, 64]); maskM_d = DI("maskM", [128, 128]); iota_d = DI("iota_n", [128, NT])
    kv_d = DI("kvals", [128, 16]); kvr_d = DI("kvals_r", [128, 16])
    w_in_d = DI("w_in", [1024, 1408]); w_out_d = DI("w_out", [1024, 1024])
    w_up_d = DI("w_up", [1024, 4096]); w_down_d = DI("w_down", [4096, 1024]); w_glu_d = DI("w_glu", [512, 512])
    gpre_d = DI("gpre", [128, 8]); gcat_d = DI("gcat", [128, 8]); gmlp_d = DI("gmlp", [128, 8])
    gpost_d = DI("gpost", [128, 1024]); gmpost_d = DI("gmpost", [128, 1024])
    sink_d = DI("sinks", [128, 8]); dvec_d = DI("dvec", [128, 4])
    art_d = DI("art", [128, 32]); ait_d = DI("ait", [128, 32]); ldt_d = DI("ldt", [128, 32])
    btr_d = DI("btr", [128, 32, 16]); bti_d = DI("bti", [128, 32, 16])
    ctr_d = DI("ctr", [128, 32, 16]); cti_d = DI("cti", [128, 32, 16])
    ckd_d = DI("ckd", [16, 128, 256]); ck_d = DI("ck", [16, 128, 128]); cv_d = DI("cv", [16, 128, 128])
    h0s_d = DI("h0s", [128, 32, 16]); h0d_d = DI("h0d", [128, 32, 16])

    y_main = DO("y_main", [2048, 1024]); y_s = DO("y_s", [64, 1024])
    kwin = DO("kwin", [128, 128]); vwin = DO("vwin", [128, 128]); hend_o = DO("hend", [128, 32])
    kc_s = DO("kc_s", [16, 124, 128]); vc_s = DO("vc_s", [16, 124, 128])
    knew = DO("knew", [64, 128]); vnew = DO("vnew", [64, 128]); hs_new = DO("hs_new", [128, 32, 16])

    Ud = DS("Ud", [4, 128, 8, NT], BF16); Usd = DS("Usd", [4, 128, 4, 16], BF16)
    Yd = DS("Yd", [8, 512, NMAIN], BF16); Ysd = DS("Ysd", [4, 512, 16], BF16)

    with ExitStack() as top:
        k = K(nc, top)

        def SB(st, name, shape, dt=F32):
            return st.enter_context(nc.sbuf_tensor("sb_" + name, list(shape), dt)), Res()

        def PS(st, name, shape, dt=F32):
            n = int(np.prod(shape[1:]))
            per_bank = 512 if dt == F32 else 1024
            nb = -(-n // per_bank)
            t = st.enter_context(nc.psum_tensor("ps_" + name, [128, nb * per_bank], dt))
            v = t[:, 0:n]
            if len(shape) > 2:
                names = ["d%d" % i for i in range(len(shape) - 1)]
                pat = "p (%s) -> p %s" % (" ".join(names), " ".join(names))
                v = v.rearrange(pat, **{nm: int(sz) for nm, sz in zip(names, shape[1:])})
            return v, Res()

        identf, r_identf = SB(top, "identf", [128, 128])
        ident, r_ident = SB(top, "ident", [128, 128], BF16)
        k.op("pool", lambda e: e.memset(identf[:, :], 0.0), writes=[r_identf])
        k.op("pool", lambda e: e.affine_select(identf[:, :], identf[:, :], [[1, 128]], ALU.not_equal, 1.0,
                                               base=0, channel_multiplier=-1),
             reads=[r_identf], writes=[r_identf])
        k.op("dve", lambda e: e.tensor_copy(ident[:, :], identf[:, :]), reads=[r_identf], writes=[r_ident])
        onesb, r_onesb = SB(top, "onesb", [128, 128], BF16)
        k.op("pool", lambda e: e.memset(onesb[:, :], 1.0), writes=[r_onesb])

        def ld(st, name, dram, shape, dt=F32, q="sp"):
            t, r = SB(st, name, shape, dt)
            idx = tuple(slice(None) for _ in shape)
            k.dma(q, t[idx], dram, writes=[r])
            return t, r

        epsb, r_epsb = SB(top, "epsb", [128, 1])
        k.op("pool", lambda e: e.memset(epsb[:, :], EPS), writes=[r_epsb])
        hpi, r_hpi = SB(top, "hpi", [128, 1])
        k.op("pool", lambda e: e.memset(hpi[:, :], math.pi / 2), writes=[r_hpi])
        catT, r_catT = SB(top, "catT", [128, 8, NCOL], BF16)

        stAC = ExitStack()
        top.enter_context(stAC)
        Mbf, r_Mbf = SB(stAC, "Mbf", [128, 32, 128], BF16)
        Pbf, r_Pbf = SB(stAC, "Pbf", [128, 32, 128], BF16)
        Pdbf, r_Pdbf = SB(stAC, "Pdbf", [128, 32, 128], BF16)
        Qbf, r_Qbf = SB(stAC, "Qbf", [128, 32, 128], BF16)
        R8, r_R8 = SB(stAC, "R8", [128, 32])
        TH8, r_TH8 = SB(stAC, "TH8", [128, 32])
        E4r, r_E4 = SB(stAC, "E4r", [128, 32])
        E4i, _ = SB(stAC, "E4i", [128, 32])
        UX, _ = SB(stAC, "UX", [128, 4 * 8 * NT], BF16)
        UT = UX[:, :].rearrange("p (c s n) -> p c s n", c=4, s=8); r_UT = Res()
        USJ = UX[:, :].rearrange("p (g n) -> p g n", g=32)
        r_USJl = [Res() for _ in range(8)]
        stAB = ExitStack()
        top.enter_context(stAB)
        qT, r_qT = SB(stAB, "qT", [128, 17, 4, 128], BF16)
        kT, r_kT = SB(stAB, "kT", [128, 18, 2, 128], BF16)
        Vall, r_V = SB(stAB, "Vall", [128, 18, 2, 65], BF16)
        k.op("pool", lambda e: e.memset(Vall[:, :, :, :], 1.0), writes=[r_V])
        k.op("pool", lambda e: e.memset(Vall[64:128, 17, :, 0:64], 0.0), writes=[r_V])

        stA = ExitStack()
        with stA:
            win, r_win = SB(stA, "win", [128, 8, 1408], BF16)
            k.dma("pool", win[:, :, :], w_in_d.rearrange("(c p) n -> p c n", p=128), writes=[r_win])
            gpre, r_gpre = ld(stA, "gpre", gpre_d, [128, 8])
            pos, r_pos = ld(stA, "pos", pos_d, [128, 18])
            invf, r_invf = ld(stA, "invf", invf_d, [128, 8])
            rt, r_rt = SB(stA, "rt", [128, 18, 8])
            rti, r_rti = SB(stA, "rti", [128, 18, 8], I32)
            rfr, r_rfr = SB(stA, "rfr", [128, 18, 8])
            rab, r_rab = SB(stA, "rab", [128, 18, 8])
            rsin, r_rsin = SB(stA, "rsin", [128, 18, 8])
            rcos, r_rcos = SB(stA, "rcos", [128, 18, 8])
            k.op("dve", lambda e: e.tensor_tensor(rt[:, :, :], bc(pos[:, :], 2, [128, 18, 8]),
                                                  bc(invf[:, :], 1, [128, 18, 8]), ALU.mult),
                 reads=[r_pos, r_invf], writes=[r_rt])
            k.op("dve", lambda e: e.tensor_copy(rti[:, :, :], rt[:, :, :]), reads=[r_rt], writes=[r_rti])
            k.op("dve", lambda e: e.tensor_tensor(rfr[:, :, :], rt[:, :, :], rti[:, :, :], ALU.subtract),
                 reads=[r_rt, r_rti], writes=[r_rfr])
            k.op("act", lambda e: e.activation(rsin[:, :, :], rfr[:, :, :], AF.Sin, scale=TWO_PI_S),
                 reads=[r_rfr], writes=[r_rsin])
            k.op("act", lambda e: e.activation(rab[:, :, :], rfr[:, :, :], AF.Abs), reads=[r_rfr], writes=[r_rab])
            k.op("act", lambda e: e.activation(rcos[:, :, :], rab[:, :, :], AF.Sin, scale=-TWO_PI_S, bias=hpi[:, :]),
                 reads=[r_rab, r_hpi], writes=[r_rcos])

            stP = ExitStack()
            with stP:
                art, r_art = ld(stP, "art", art_d, [128, 32])
                ait, r_ait = ld(stP, "ait", ait_d, [128, 32])
                ldt, r_ldt = ld(stP, "ldt", ldt_d, [128, 32])
                btr, r_btr = ld(stP, "btr", btr_d, [128, 32, 16])
                bti, r_bti = ld(stP, "bti", bti_d, [128, 32, 16])
                ctr, r_ctr = ld(stP, "ctr", ctr_d, [128, 32, 16])
                cti, r_cti = ld(stP, "cti", cti_d, [128, 32, 16])
                kv, r_kv = ld(stP, "kv", kv_d, [128, 16])
                kvr, r_kvr = ld(stP, "kvr", kvr_d, [128, 16])
                maskM, r_maskM = ld(stP, "maskM", maskM_d, [128, 128])
                dt_, r_dt = SB(stP, "dt_", [128, 32])
                ardt, r_ardt = SB(stP, "ardt", [128, 32])
                aidt, r_aidt = SB(stP, "aidt", [128, 32])
                k.op("act", lambda e: e.activation(dt_[:, :], ldt[:, :], AF.Exp), reads=[r_ldt], writes=[r_dt])
                k.op("pool", lambda e: e.tensor_tensor(ardt[:, :], art[:, :], dt_[:, :], ALU.mult),
                     reads=[r_art, r_dt], writes=[r_ardt])
                k.op("pool", lambda e: e.tensor_tensor(aidt[:, :], ait[:, :], dt_[:, :], ALU.mult),
                     reads=[r_ait, r_dt], writes=[r_aidt])
                k.op("pool", lambda e: e.tensor_scalar(TH8[:, :], aidt[:, :], 8.0 / (2 * math.pi), None, ALU.mult),
                     reads=[r_aidt], writes=[r_TH8])

                def power_table(name, kvt, r_kvt):
                    Er, r_Er = SB(stP, name + "r", [128, 32, 16])
                    Ei, r_Ei = SB(stP, name + "i", [128, 32, 16])
                    mg, r_mg = SB(stP, name + "m", [128, 32, 16])
                    tq, r_tq = SB(stP, name + "t", [128, 32, 16])
                    tqi, r_tqi = SB(stP, name + "ti", [128, 32, 16], I32)
                    fr, r_fr = SB(stP, name + "f", [128, 32, 16])
                    ab, r_ab = SB(stP, name + "a", [128, 32, 16])
                    sh = [128, 32, 16]
                    k.op("pool", lambda e: e.tensor_tensor(mg[:, :, :], bc(ardt[:, :], 2, sh), bc(kvt[:, :], 1, sh), ALU.mult),
                         reads=[r_ardt, r_kvt], writes=[r_mg])
                    k.op("act", lambda e: e.activation(mg[:, :, :], mg[:, :, :], AF.Exp), reads=[r_mg], writes=[r_mg])
                    k.op("pool", lambda e: e.tensor_tensor(tq[:, :, :], bc(aidt[:, :], 2, sh), bc(kvt[:, :], 1, sh), ALU.mult),
                         reads=[r_aidt, r_kvt], writes=[r_tq])
                    k.op("dve", lambda e: e.tensor_scalar(tq[:, :, :], tq[:, :, :], 1.0 / (2 * math.pi), None, ALU.mult),
                         reads=[r_tq], writes=[r_tq])
                    k.op("dve", lambda e: e.tensor_copy(tqi[:, :, :], tq[:, :, :]), reads=[r_tq], writes=[r_tqi])
                    k.op("dve", lambda e: e.tensor_tensor(fr[:, :, :], tq[:, :, :], tqi[:, :, :], ALU.subtract),
                         reads=[r_tq, r_tqi], writes=[r_fr])
                    k.op("act", lambda e: e.activation(Ei[:, :, :], fr[:, :, :], AF.Sin, scale=TWO_PI_S),
                         reads=[r_fr], writes=[r_Ei])
                    k.op("act", lambda e: e.activation(ab[:, :, :], fr[:, :, :], AF.Abs), reads=[r_fr], writes=[r_ab])
                    k.op("act", lambda e: e.activation(Er[:, :, :], ab[:, :, :], AF.Sin, scale=-TWO_PI_S, bias=hpi[:, :]),
                         reads=[r_ab, r_hpi], writes=[r_Er])
                    k.op("pool", lambda e: e.tensor_tensor(Er[:, :, :], Er[:, :, :], mg[:, :, :], ALU.mult),
                         reads=[r_mg, r_Er], writes=[r_Er])
                    k.op("pool", lambda e: e.tensor_tensor(Ei[:, :, :], Ei[:, :, :], mg[:, :, :], ALU.mult),
                         reads=[r_mg, r_Ei], writes=[r_Ei])
                    return Er, Ei, r_Er, r_Ei, mg, r_mg

                Er, Ei, r_Er, r_Ei, mg, r_mg = power_table("E", kv, r_kv)
                Vr, Vi, r_Vr, r_Vi, _, _ = power_table("V", kvr, r_kvr)
                k.op("pool", lambda e: e.tensor_copy(R8[:, :], mg[:, :, 15]), reads=[r_mg], writes=[r_R8])
                k.op("pool", lambda e: e.tensor_copy(E4r[:, :], Er[:, :, 11]), reads=[r_Er], writes=[r_E4])
                k.op("pool", lambda e: e.tensor_copy(E4i[64:128, :], Ei[64:128, :, 11]), reads=[r_Ei], writes=[r_E4])
                k.op("pool", lambda e: e.tensor_scalar(E4i[0:64, :], Ei[0:64, :, 11], -1.0, None, ALU.mult),
                     reads=[r_Ei], writes=[r_E4])
                nr, r_nr = SB(stP, "nr", [128, 32]); den, r_den = SB(stP, "den", [128, 32])
                t1, r_t1 = SB(stP, "t1", [128, 32]); t2, r_t2 = SB(stP, "t2", [128, 32])
                cfr, r_cfr = SB(stP, "cfr", [128, 32]); cfi, r_cfi = SB(stP, "cfi", [128, 32])
                P_ = "pool"
                k.op(P_, lambda e: e.tensor_scalar(nr[:, :], Er[:, :, 8], -1.0, None, ALU.add), reads=[r_Er], writes=[r_nr])
                k.op(P_, lambda e: e.tensor_tensor(t1[:, :], art[:, :], art[:, :], ALU.mult), reads=[r_art], writes=[r_t1])
                k.op(P_, lambda e: e.tensor_tensor(t2[:, :], ait[:, :], ait[:, :], ALU.mult), reads=[r_ait], writes=[r_t2])
                k.op(P_, lambda e: e.tensor_tensor(den[:, :], t1[:, :], t2[:, :], ALU.add), reads=[r_t1, r_t2], writes=[r_den])
                k.op("dve", lambda e: e.reciprocal(den[:, :], den[:, :]), reads=[r_den], writes=[r_den])
                k.op(P_, lambda e: e.tensor_tensor(t1[:, :], nr[:, :], art[:, :], ALU.mult), reads=[r_nr, r_art], writes=[r_t1])
                k.op(P_, lambda e: e.tensor_tensor(t2[:, :], Ei[:, :, 8], ait[:, :], ALU.mult), reads=[r_Ei, r_ait], writes=[r_t2])
                k.op(P_, lambda e: e.tensor_tensor(t1[:, :], t1[:, :], t2[:, :], ALU.add), reads=[r_t1, r_t2], writes=[r_t1])
                k.op(P_, lambda e: e.tensor_tensor(cfr[:, :], t1[:, :], den[:, :], ALU.mult), reads=[r_t1, r_den], writes=[r_cfr])
                k.op(P_, lambda e: e.tensor_tensor(t1[:, :], Ei[:, :, 8], art[:, :], ALU.mult), reads=[r_Ei, r_art], writes=[r_t1])
                k.op(P_, lambda e: e.tensor_tensor(t2[:, :], nr[:, :], ait[:, :], ALU.mult), reads=[r_nr, r_ait], writes=[r_t2])
                k.op(P_, lambda e: e.tensor_tensor(t1[:, :], t1[:, :], t2[:, :], ALU.subtract), reads=[r_t1, r_t2], writes=[r_t1])
                k.op(P_, lambda e: e.tensor_tensor(cfi[:, :], t1[:, :], den[:, :], ALU.mult), reads=[r_t1, r_den], writes=[r_cfi])
                Br, r_Br = SB(stP, "Br", [128, 32, 16]); Bi, r_Bi = SB(stP, "Bi", [128, 32, 16])
                u1, r_u1 = SB(stP, "u1", [128, 32, 16]); u2, r_u2 = SB(stP, "u2", [128, 32, 16])
                s3 = [128, 32, 16]
                k.op(P_, lambda e: e.tensor_tensor(u1[:, :, :], btr[:, :, :], bc(cfr[:, :], 2, s3), ALU.mult), reads=[r_btr, r_cfr], writes=[r_u1])
                k.op(P_, lambda e: e.tensor_tensor(u2[:, :, :], bti[:, :, :], bc(cfi[:, :], 2, s3), ALU.mult), reads=[r_bti, r_cfi], writes=[r_u2])
                k.op(P_, lambda e: e.tensor_tensor(Br[:, :, :], u1[:, :, :], u2[:, :, :], ALU.subtract), reads=[r_u1, r_u2], writes=[r_Br])
                k.op(P_, lambda e: e.tensor_tensor(u1[:, :, :], bti[:, :, :], bc(cfr[:, :], 2, s3), ALU.mult), reads=[r_bti, r_cfr], writes=[r_u1])
                k.op(P_, lambda e: e.tensor_tensor(u2[:, :, :], btr[:, :, :], bc(cfi[:, :], 2, s3), ALU.mult), reads=[r_btr, r_cfi], writes=[r_u2])
                k.op(P_, lambda e: e.tensor_tensor(Bi[:, :, :], u1[:, :, :], u2[:, :, :], ALU.add), reads=[r_u1, r_u2], writes=[r_Bi])

                s4 = [128, 8, 8, 16]
                def uxf(i):
                    v = UX[:, i * 2048:(i + 1) * 2048].bitcast(F32)
                    return v.rearrange("p (g s j) -> p g s j", g=8, s=8), Res()
                (p1, r_p1), (p2, r_p2), (p3, r_p3), (p4, r_p4) = uxf(0), uxf(1), uxf(2), uxf(3)
                (Zs, r_Zs), (Zd, r_Zd), (Ws, r_Ws) = uxf(4), uxf(5), uxf(6)
                ppA, r_ppA = PS(stP, "ppA", [128, 4, 128]); ppB, r_ppB = PS(stP, "ppB", [128, 4, 128])
                ppC, r_ppC = PS(stP, "ppC", [128, 4, 128])
                LO = slice(0, 64); HI = slice(64, 128)
                for gb in range(4):
                    gs = slice(gb * 8, gb * 8 + 8)

                    def cprod(Tr, Ti, r_Tr, r_Ti, ksl, Xr, Xi, r_Xr, r_Xi):
                        er = bc(Tr[:, gs, ksl], 3, s4); ei = bc(Ti[:, gs, ksl], 3, s4)
                        xr = bc(Xr[:, gs, :], 2, s4); xi = bc(Xi[:, gs, :], 2, s4)
                        k.op("dve", lambda e: e.tensor_tensor(p1[:, :, :, :], er, xr, ALU.mult), reads=[r_Tr, r_Xr], writes=[r_p1])
                        k.op("dve", lambda e: e.tensor_tensor(p2[:, :, :, :], ei, xi, ALU.mult), reads=[r_Ti, r_Xi], writes=[r_p2])
                        k.op("dve", lambda e: e.tensor_tensor(p3[:, :, :, :], er, xi, ALU.mult), reads=[r_Tr, r_Xi], writes=[r_p3])
                        k.op(P_, lambda e: e.tensor_tensor(p4[:, :, :, :], ei, xr, ALU.mult), reads=[r_Ti, r_Xr], writes=[r_p4])

                    cprod(Vr, Vi, r_Vr, r_Vi, slice(1, 9), Br, Bi, r_Br, r_Bi)
                    k.op("dve", lambda e: e.tensor_tensor(Zs[LO], p1[LO], p2[LO], ALU.subtract), reads=[r_p1, r_p2], writes=[r_Zs])
                    k.op("dve", lambda e: e.tensor_tensor(Zs[HI], p3[HI], p4[HI], ALU.add), reads=[r_p3, r_p4], writes=[r_Zs])
                    k.op("dve", lambda e: e.tensor_tensor(Zd[LO], p3[LO], p4[LO], ALU.add), reads=[r_p3, r_p4], writes=[r_Zd])
                    k.op("dve", lambda e: e.tensor_tensor(Zd[HI], p2[HI], p1[HI], ALU.subtract), reads=[r_p1, r_p2], writes=[r_Zd])
                    cprod(Er, Ei, r_Er, r_Ei, slice(0, 8), ctr, cti, r_ctr, r_cti)
                    k.op("dve", lambda e: e.tensor_tensor(Ws[LO], p1[LO], p2[LO], ALU.subtract), reads=[r_p1, r_p2], writes=[r_Ws])
                    k.op("dve", lambda e: e.scalar_tensor_tensor(Ws[HI], p3[HI], -1.0, p4[HI], ALU.mult, ALU.subtract),
                         reads=[r_p3, r_p4], writes=[r_Ws])
                    for hb in range(2):
                        for gi in range(4):
                            g8 = hb * 4 + gi
                            zs = Zs[:, g8, :, :].rearrange("p s j -> p (s j)")
                            zd = Zd[:, g8, :, :].rearrange("p s j -> p (s j)")
                            ws = Ws[:, g8, :, :].rearrange("p s j -> p (s j)")
                            k.op("pe", lambda e: e.transpose(ppA[:, gi, :], zs, identf[:, :]),
                                 reads=[r_Zs, r_identf], writes=[r_ppA])
                            k.op("pe", lambda e: e.transpose(ppB[:, gi, :], zd, identf[:, :]),
                                 reads=[r_Zd, r_identf], writes=[r_ppB])
                            k.op("pe", lambda e: e.matmul(ppC[:, gi, :], zs, ws, start=True, stop=True),
                                 reads=[r_Zs, r_Ws], writes=[r_ppC])
                        g0 = gb * 8 + hb * 4
                        k.op("act", lambda e: e.copy(Pbf[:, g0:g0 + 4, :], ppA[:, :, :]), reads=[r_ppA], writes=[r_Pbf])
                        k.op("act", lambda e: e.copy(Pdbf[:, g0:g0 + 4, :], ppB[:, :, :]), reads=[r_ppB], writes=[r_Pdbf])
                        k.op("dve", lambda e: e.tensor_tensor(Mbf[:, g0:g0 + 4, :], ppC[:, :, :],
                                                              bc(maskM[:, :], 1, [128, 4, 128]), ALU.mult),
                             reads=[r_ppC, r_maskM], writes=[r_Mbf])
                    cprod(Er, Ei, r_Er, r_Ei, slice(8, 16), ctr, cti, r_ctr, r_cti)
                    qv = Qbf[:, gs, :].rearrange("p g (t c) -> p g t c", c=16)
                    k.op("dve", lambda e: e.tensor_tensor(qv[LO], p1[LO], p2[LO], ALU.subtract), reads=[r_p1, r_p2], writes=[r_Qbf])
                    k.op("dve", lambda e: e.scalar_tensor_tensor(qv[HI], p3[HI], -1.0, p4[HI], ALU.mult, ALU.subtract),
                         reads=[r_p3, r_p4], writes=[r_Qbf])
                k.barrier()
            for c in range(8):
                k.op("dve", lambda e: e.tensor_scalar(win[:, c, :], win[:, c, :], gpre[:, c:c + 1], None, ALU.mult),
                     reads=[r_gpre, r_win], writes=[r_win])
            UTs, r_UTs = SB(stA, "UTs", [128, 4, 4, 16], BF16)
            xt = [SB(stA, "xt%d" % i, [128, 1024]) for i in range(2)]
            junk, r_junk = SB(stA, "junk", [128, 1024], BF16)
            ss = [SB(stA, "ss%d" % i, [128, 1]) for i in range(2)]
            xn = [SB(stA, "xn%d" % i, [128, 1024], BF16) for i in range(2)]
            hT = [SB(stA, "hT%d" % i, [128, 8, 128], BF16) for i in range(2)]
            qkf = [SB(stA, "qkf%d" % i, [128, 896]) for i in range(2)]
            rtmp, r_rtmp = SB(stA, "rtmp", [128, 4, 12, 8])
            qkb, r_qkb = SB(stA, "qkb", [128, 768], BF16)
            tpp = [PS(stA, "tpp%d" % i, [128, 8, 128], BF16) for i in range(2)]
            pq, r_pq = PS(stA, "pq", [128, 512]); pkv, r_pkv = PS(stA, "pkv", [128, 384])
            pu, r_pu = PS(stA, "pu", [128, 4, 128]); pqt, r_pqt = PS(stA, "pqt", [128, 6, 128], BF16)

            tiles = [("p", i) for i in range(NTILE_P)] + [("m", i) for i in range(NTILE_M)] + [("s", 0)]
            def tile_body(it, kind, ti):
                b2 = it % 2
                np_ = 64 if kind == "s" else 128
                PSL = slice(0, np_)
                src = {"p": xp, "m": xm, "s": xs}[kind]
                (xt_, r_xt), (ss_, r_ss), (xn_, r_xn), (hT_, r_hT), (qkf_, r_qkf), (tp_, r_tp) = \
                    xt[b2], ss[b2], xn[b2], hT[b2], qkf[b2], tpp[b2]
                k.dma("sp", xt_[PSL, :], src[ti * 128: ti * 128 + np_, :], writes=[r_xt])
                k.op("act", lambda e: e.activation(junk[PSL, :], xt_[PSL, :], AF.Square, accum_out=ss_[PSL, :]),
                     reads=[r_xt], writes=[r_junk, r_ss])
                k.op("act", lambda e: e.activation(ss_[PSL, :], ss_[PSL, :], AF.Sqrt, bias=EPS, scale=1.0 / 1024),
                     reads=[r_ss], writes=[r_ss])
                k.op("dve", lambda e: e.reciprocal(ss_[PSL, :], ss_[PSL, :]), reads=[r_ss], writes=[r_ss])
                yield
                k.op("act", lambda e: e.activation(xn_[PSL, :], xt_[PSL, :], AF.Copy, scale=ss_[PSL, :]),
                     reads=[r_xt, r_ss], writes=[r_xn])
                for c in range(8):
                    k.op("pe", lambda e: e.transpose(tp_[:, c, PSL], xn_[PSL, c * 128:(c + 1) * 128], ident[PSL, PSL]),
                         reads=[r_xn, r_ident], writes=[r_tp], sig=SG("A_tp", c == 7))
                k.op("dve", lambda e: e.tensor_copy(hT_[:, :, PSL], tp_[:, :, PSL]), reads=[r_tp], writes=[r_hT])
                yield
                for ch in range(4):
                    for c in range(8):
                        k.op("pe", lambda e: e.matmul(pu[:, ch, PSL], win[:, c, 896 + ch * 128: 896 + (ch + 1) * 128],
                                                      hT_[:, c, PSL], start=(c == 0), stop=(c == 7)),
                             reads=[r_win, r_hT], writes=[r_pu], sig=SG("A_u", c == 7 and ch == 3), chain=CH("A_u", c > 0))
                if kind != "p":
                    for c in range(8):
                        k.op("pe", lambda e: e.matmul(pq[PSL, :], hT_[:, c, PSL], win[:, c, 0:512], start=(c == 0), stop=(c == 7)),
                             reads=[r_win, r_hT], writes=[r_pq], sig=SG("A_q", c == 7), chain=CH("A_q", c > 0))
                    for c in range(8):
                        k.op("pe", lambda e: e.matmul(pkv[PSL, :], hT_[:, c, PSL], win[:, c, 512:896], start=(c == 0), stop=(c == 7)),
                             reads=[r_win, r_hT], writes=[r_pkv], sig=SG("A_kv", c == 7), chain=CH("A_kv", c > 0))
                yield
                if kind == "s":
                    k.op("dve", lambda e: e.tensor_copy(UTs[:, :, :, :].rearrange("p c t b -> p c b t"),
                                                        pu[:, :, 0:64].rearrange("p c (b t) -> p c b t", t=4)),
                         reads=[r_pu], writes=[r_UTs])
                else:
                    n0 = (ti * 16) if kind == "p" else (NPRE + ti * 16)
                    k.op("dve", lambda e: e.tensor_copy(UT[:, :, :, n0:n0 + 16].rearrange("p c s n -> p c n s"),
                                                        pu[:, :, :].rearrange("p c (n s) -> p c n s", s=8)),
                         reads=[r_pu], writes=[r_UT])
                if kind == "p":
                    return
                k.op("act", lambda e: e.copy(qkf_[PSL, 0:512], pq[PSL, :]), reads=[r_pq], writes=[r_qkf])
                k.op("act", lambda e: e.copy(qkf_[PSL, 512:896], pkv[PSL, :]), reads=[r_pkv], writes=[r_qkf])
                yield
                pi = 17 if kind == "s" else ti
                qv = qkf_[PSL, 0:768].rearrange("p (h d) -> p h d", d=64)
                cs_ = bc(rcos[PSL, pi, :], 1, [np_, 12, 8]); sn_ = bc(rsin[PSL, pi, :], 1, [np_, 12, 8])
                k.op("dve", lambda e: e.tensor_tensor(rtmp[PSL, 0], qv[:, :, 0:8], cs_, ALU.mult), reads=[r_qkf, r_rcos], writes=[r_rtmp])
                k.op("dve", lambda e: e.tensor_tensor(rtmp[PSL, 1], qv[:, :, 8:16], sn_, ALU.mult), reads=[r_qkf, r_rsin], writes=[r_rtmp])
                k.op("pool", lambda e: e.tensor_tensor(rtmp[PSL, 2], qv[:, :, 8:16], cs_, ALU.mult), reads=[r_qkf, r_rcos], writes=[r_rtmp])
                k.op("pool", lambda e: e.tensor_tensor(rtmp[PSL, 3], qv[:, :, 0:8], sn_, ALU.mult), reads=[r_qkf, r_rsin], writes=[r_rtmp])
                k.op("dve", lambda e: e.tensor_tensor(qv[:, :, 0:8], rtmp[PSL, 0], rtmp[PSL, 1], ALU.subtract), reads=[r_rtmp], writes=[r_qkf])
                k.op("dve", lambda e: e.tensor_tensor(qv[:, :, 8:16], rtmp[PSL, 2], rtmp[PSL, 3], ALU.add), reads=[r_rtmp], writes=[r_qkf])
                k.op("act", lambda e: e.copy(qkb[PSL, :], qkf_[PSL, 0:768]), reads=[r_qkf], writes=[r_qkb])
                k.op("pool", lambda e: e.tensor_copy(Vall[PSL, pi, :, 0:64], qkf_[PSL, 768:896].rearrange("p (a d) -> p a d", d=64)),
                     reads=[r_qkf], writes=[r_V])
                yield
                for j in range(2):
                    k.op("pe", lambda e: e.transpose(pqt[:, j, PSL], qkb[PSL, 512 + j * 128: 512 + (j + 1) * 128], ident[PSL, PSL]),
                         reads=[r_qkb, r_ident], writes=[r_pqt], sig=SG("A_kt", j == 1))
                k.op("dve", lambda e: e.tensor_copy(kT[:, pi, :, PSL], pqt[:, 0:2, PSL]), reads=[r_pqt], writes=[r_kT])
                if not (kind == "m" and ti == 0):
                    qi = 16 if kind == "s" else ti - 1
                    for j in range(4):
                        k.op("pe", lambda e: e.transpose(pqt[:, 2 + j, PSL], qkb[PSL, j * 128:(j + 1) * 128], ident[PSL, PSL]),
                             reads=[r_qkb, r_ident], writes=[r_pqt], sig=SG("A_qt", j == 3))
                    k.op("dve", lambda e: e.tensor_copy(qT[:, qi, :, PSL], pqt[:, 2:6, PSL]), reads=[r_pqt], writes=[r_qT])
                if kind == "m" and ti == NTILE_M - 1:
                    k.dma("sp", kwin[:, 0:64], qkf_[:, 512:576], reads=[r_qkf])
                    k.dma("sp", kwin[:, 64:128], qkf_[:, 640:704], reads=[r_qkf])
                    k.dma("sp", vwin[:, :], qkf_[:, 768:896], reads=[r_qkf])
                if kind == "s":
                    k.dma("sp", knew[:, 0:64], qkf_[0:64, 512:576], reads=[r_qkf])
                    k.dma("sp", knew[:, 64:128], qkf_[0:64, 640:704], reads=[r_qkf])
                    k.dma("sp", vnew[:, :], qkf_[0:64, 768:896], reads=[r_qkf])
            run_staged([tile_body(it, kind, ti) for it, (kind, ti) in enumerate(tiles)], 6, order=[3, 5, 2, 4, 1, 0])
            r_Udl = [Res() for _ in range(4)]; r_Usdl = [Res() for _ in range(4)]
            for ch in range(4):
                k.dma("sp", Ud[ch], UT[:, ch, :, :], reads=[r_UT], writes=[r_Udl[ch]])
                k.dma("sp", Usd[ch], UTs[:, ch, :, :], reads=[r_UTs], writes=[r_Usdl[ch]])
            k.barrier()

        stB = ExitStack()
        with stB:
            mask1, r_m1 = SB(stB, "mask1f", [128, 128]); maskp, r_mp = SB(stB, "maskpf", [128, 128])
            maskc, r_mc = SB(stB, "maskcf", [128, 128]); masks, r_ms = SB(stB, "masksf", [128, 64])
            k.dma("sp", mask1[:, :], mask1_d, writes=[r_m1]); k.dma("sp", maskp[:, :], maskp_d, writes=[r_mp])
            k.dma("sp", maskc[:, :], maskc_d, writes=[r_mc]); k.dma("sp", masks[:, :], masks_d, writes=[r_ms])
            mk = {}
            for nm, (t_, r_) in {"1": (mask1, r_m1), "p": (maskp, r_mp), "c": (maskc, r_mc)}.items():
                tb, rb = SB(stB, "mb" + nm, [128, 128], BF16)
                k.op("dve", lambda e: e.tensor_copy(tb[:, :], t_[:, :]), reads=[r_], writes=[rb])
                mk[nm] = (tb, rb)
            msb, r_msb = SB(stB, "msb", [128, 64], BF16)
            k.op("dve", lambda e: e.tensor_copy(msb[:, :], masks[:, :]), reads=[r_ms], writes=[r_msb])
            esink, r_esink = ld(stB, "esink", sink_d, [128, 8])
            k.op("act", lambda e: e.activation(esink[:, :], esink[:, :], AF.Exp), reads=[r_esink], writes=[r_esink])
            k.wait_all("sp", [r_UT])
            Uv0 = Ud.rearrange("c (g j) s n -> s j (c g) n", j=16)
            for s in range(8):
                k.dma("sp", USJ[16 * s:16 * s + 16, :, :], Uv0[s], reads=r_Udl, writes=[r_USJl[s]])
            PT = [SB(stB, "PT%d" % i, [128, 2, 2, 2, 128], BF16) for i in range(4)]
            LT = [PS(stB, "LT%d" % i, [128, 2, 2, 2, 128]) for i in range(2)]
            Op = [PS(stB, "Op%d" % i, [128, 4, 65]) for i in range(2)]
            ptr, r_ptr = PS(stB, "ptr", [128, 4, 128], BF16)
            attl = [SB(stB, "att%d" % i, [128, 8, 64]) for i in range(2)]
            attb, r_attb = SB(stB, "attb", [128, 512], BF16)
            dnl = [SB(stB, "dn%d" % i, [128, 8]) for i in range(2)]
            ssa, r_ssa = SB(stB, "ssa", [128, 1])
            junkb, r_junkb = SB(stB, "junkb", [128, 512], BF16)

            def attn_evac(npart, Ops, par):
                PSL = slice(0, npart)
                (att, r_att), (dn, r_dn) = attl[par], dnl[par]
                for kvh in range(2):
                    o_, r_o = Ops[kvh]
                    hs = slice(kvh * 4, kvh * 4 + 4)
                    k.op("dve", lambda e: e.tensor_tensor(dn[PSL, hs], o_[PSL, :, 64], esink[PSL, hs], ALU.add),
                         reads=[r_o, r_esink], writes=[r_dn])
                    k.op("dve", lambda e: e.reciprocal(dn[PSL, hs], dn[PSL, hs]), reads=[r_dn], writes=[r_dn])
                    k.op("dve", lambda e: e.tensor_tensor(att[PSL, hs, :], o_[PSL, :, 0:64], bc(dn[PSL, hs], 2, [npart, 4, 64]), ALU.mult),
                         reads=[r_o, r_dn], writes=[r_att])

            def attn_norm(npart, col0, par):
                PSL = slice(0, npart)
                (att, r_att) = attl[par]
                av = att[PSL, :, :].rearrange("p h d -> p (h d)")
                k.op("act", lambda e: e.activation(junkb[PSL, :], av, AF.Square, accum_out=ssa[PSL, :]),
                     reads=[r_att], writes=[r_junkb, r_ssa])
                k.op("act", lambda e: e.activation(ssa[PSL, :], ssa[PSL, :], AF.Ln, bias=epsb[PSL, :], scale=1.0 / 512),
                     reads=[r_ssa, r_epsb], writes=[r_ssa])
                k.op("act", lambda e: e.activation(ssa[PSL, :], ssa[PSL, :], AF.Exp, scale=-0.5), reads=[r_ssa], writes=[r_ssa])
                k.op("act", lambda e: e.activation(attb[PSL, :], av, AF.Copy, scale=ssa[PSL, :]),
                     reads=[r_att, r_ssa], writes=[r_attb])
                for j in range(4):
                    k.op("pe", lambda e: e.transpose(ptr[:, j, PSL], attb[PSL, j * 128:(j + 1) * 128], ident[PSL, PSL]),
                         reads=[r_attb, r_ident], writes=[r_ptr], sig=SG("B_tr", j == 3))
                k.op("dve", lambda e: e.tensor_copy(catT[:, 0:4, col0:col0 + npart], ptr[:, :, PSL]),
                     reads=[r_ptr], writes=[r_catT])

            def finish_attention(npart, Ops, col0):
                attn_evac(npart, Ops, 0)
                attn_norm(npart, col0, 0)

            def attn_block(blk):
                pts = []
                for kvh in range(2):
                    (lt, r_lt) = LT[kvh]
                    (pt, r_pt) = PT[(blk % 2) * 2 + kvh]
                    pts.append((pt, r_pt))
                    for kb in range(2):
                        for hf in range(2):
                            hs_ = slice(hf * 64, hf * 64 + 64)
                            k.op("pe", lambda e: e.matmul(lt[:, hf, kb, :, :], kT[hs_, blk - 1 + kb, kvh, :],
                                                          qT[hs_, blk - 1, 2 * kvh:2 * kvh + 2, :], start=True, stop=True),
                                 reads=[r_kT, r_qT], writes=[r_lt], sig=SG("B_lg", kb == 1 and hf == 1))
                    k.op("act", lambda e: e.activation(pt[:, :, :, :, :].rearrange("p a b c q -> p (a b c q)"),
                                                       lt.rearrange("p a b c q -> p (a b c q)"), AF.Exp, scale=0.125),
                         reads=[r_lt], writes=[r_pt])
                    for kb in range(2):
                        mt, r_mt = mk["c"] if kb == 1 else (mk["1"] if blk == 1 else mk["p"])
                        pv = pt[:, :, kb, :, :]
                        k.op("dve",
                             lambda e: e.tensor_tensor(pv, pv, mt[:, :].unsqueeze(1).unsqueeze(1).to_broadcast([128, 2, 2, 128]), ALU.mult),
                             reads=[r_mt, r_pt], writes=[r_pt])
                yield
                for kvh in range(2):
                    (pt, r_pt) = pts[kvh]
                    o_, r_o = Op[kvh]
                    for j in range(2):
                        for hf in range(2):
                            for kb in range(2):
                                k.op("pe", lambda e: e.matmul(o_[:, 2 * j + hf, :], pt[:, hf, kb, j, :], Vall[:, blk - 1 + kb, kvh, :],
                                                              start=(kb == 0), stop=(kb == 1)),
                                     reads=[r_pt, r_V], writes=[r_o], sig=SG("B_pv", j == 1 and hf == 1 and kb == 1), chain=CH("B_pv", kb == 1))
                yield
                attn_evac(128, Op, blk % 2)
                yield
                attn_norm(128, blk * 128, blk % 2)

            ab_ = [attn_block(blk) for blk in range(1, 17)]

            def aseg(i):
                try:
                    next(ab_[i])
                except StopIteration:
                    pass

            aseg(0)
            for i in range(16):
                if i + 1 < 16:
                    aseg(i + 1)
                aseg(i)
                if i >= 1:
                    aseg(i - 1)
                aseg(i)
            aseg(15)

            ckb, r_ckb = SB(stB, "ckb", [128, 16, 256], BF16)
            k.dma("pool", ckb[:, :, :], ckd_d.rearrange("b w c -> w b c"), writes=[r_ckb])
            Vc, r_Vc = SB(stB, "Vc", [128, 16, 2, 65], BF16)
            k.op("pool", lambda e: e.memset(Vc[:, :, :, :], 1.0), writes=[r_Vc])
            for a_ in range(2):
                k.dma("pool", Vc[:, :, a_, 0:64], cv_d[:, :, a_ * 64:(a_ + 1) * 64].rearrange("b w d -> w b d"), writes=[r_Vc])
            r_kcs = Res()
            k.dma("sp", kc_s[:, :, :], ck_d[:, 4:128, :], writes=[r_kcs])
            k.dma("sp", vc_s[:, :, :], cv_d[:, 4:128, :], writes=[r_kcs])
            PTn, r_PTn = SB(stB, "PTn", [128, 2, 2, 2, 64], BF16)
            k.op("pool", lambda e: e.memset(PTn[:, :, :, :, :], 0.0), writes=[r_PTn])
            ZPA, r_ZPA = SB(stB, "ZPA", [128, 16, 8, 64], BF16)
            k.op("pool", lambda e: e.memset(ZPA[:, :, :, :], 0.0), writes=[r_ZPA])
            kTcA, r_kTcA = SB(stB, "kTcA", [128, 16, 2, 128], BF16)
            ptc_all, r_ptc = PS(stB, "ptc", [128, 8, 128], BF16)
            ltn, r_ltn = LT[0]
            for kvh in range(2):
                for hf in range(2):
                    hs_ = slice(hf * 64, hf * 64 + 64)
                    for j in range(2):
                        k.op("pe", lambda e: e.matmul(ltn[0:64, hf, kvh, j, 0:64], kT[hs_, 17, kvh, 0:64],
                                                      qT[hs_, 16, 2 * kvh + j, 0:64], start=True, stop=True),
                             reads=[r_kT, r_qT], writes=[r_ltn])
            k.op("act", lambda e: e.activation(PTn[0:64].rearrange("p a b c q -> p (a b c) q"),
                                               ltn.rearrange("p a b c q -> p (a b c) q")[0:64, :, 0:64], AF.Exp, scale=0.125),
                 reads=[r_ltn], writes=[r_PTn])
            pnv = PTn[0:64].rearrange("p a b c q -> p (a b c) q")
            k.op("dve", lambda e: e.tensor_tensor(pnv, pnv, bc(msb[0:64, :], 1, [64, 8, 64]), ALU.mult),
                 reads=[r_msb, r_PTn], writes=[r_PTn])
            for kvh in range(2):
                o_, r_o = Op[kvh]
                for j in range(2):
                    for hf in range(2):
                        k.op("pe", lambda e: e.matmul(o_[0:64, 2 * j + hf, :], PTn[:, hf, kvh, j, :], Vall[:, 17, kvh, :],
                                                      start=(j == 0 and hf == 0), stop=False, skip_group_check=True),
                             reads=[r_PTn, r_V], writes=[r_o])
            ltc, r_ltc = LT[1]
            for r4 in range(4):
                for bi in range(4):
                    b = r4 * 4 + bi
                    for j in range(2):
                        k.op("pe", lambda e: e.transpose(ptc_all[:, 2 * bi + j, :], ckb[:, b, j * 128:(j + 1) * 128], ident[:, :]),
                             reads=[r_ckb, r_ident], writes=[r_ptc], sig=(bi == 3 and j == 1))
                k.op("dve", lambda e: e.tensor_copy(kTcA[:, r4 * 4:r4 * 4 + 4, :, :].rearrange("p b j w -> p (b j) w"), ptc_all[:, :, :]),
                     reads=[r_ptc], writes=[r_kTcA])
            for b in range(16):
                for kvh in range(2):
                    for hf in range(2):
                        hs_ = slice(hf * 64, hf * 64 + 64)
                        for j in range(2):
                            k.op("pe", lambda e: e.matmul(ltc[:, hf, kvh, j, 4 * b:4 * b + 4], kTcA[hs_, b, kvh, :],
                                                          qT[hs_, 16, 2 * kvh + j, 4 * b:4 * b + 4], start=True, stop=True),
                                 reads=[r_kTcA, r_qT], writes=[r_ltc], sig=(b == 15 and kvh == 1 and hf == 1 and j == 1))
            lsrc = ltc.rearrange("p a b c q -> p (a b c) q")[:, :, 0:64].rearrange("p h (b t) -> p b h t", t=4)
            zdst = bass.AP(tensor=ZPA[:, :, :, :].tensor, offset=ZPA[:, :, :, :].offset,
                           ap=[list(ZPA[:, :, :, :].ap[0]), [8 * 64 + 4, 16], [64, 8], [1, 4]])
            k.op("act", lambda e: e.activation(zdst, lsrc, AF.Exp, scale=0.125), reads=[r_ltc], writes=[r_ZPA])
            mpb, r_mpb = mk["p"]
            k.op("dve", lambda e: e.tensor_tensor(zdst, zdst, mpb[:, 0:4].unsqueeze(1).unsqueeze(1).to_broadcast([128, 16, 8, 4]), ALU.mult),
                 reads=[r_mpb, r_ZPA], writes=[r_ZPA])
            for b in range(16):
                for kvh in range(2):
                    o_, r_o = Op[kvh]
                    for j in range(2):
                        for hf in range(2):
                            k.op("pe", lambda e: e.matmul(o_[0:64, 2 * j + hf, :], ZPA[:, b, (hf * 2 + kvh) * 2 + j, :], Vc[:, b, kvh, :],
                                                          start=False, stop=(b == 15), skip_group_check=True),
                                 reads=[r_ZPA, r_Vc], writes=[r_o], sig=(b == 15 and kvh == 1 and j == 1 and hf == 1), chain=True)
            finish_attention(64, Op, 2176)
            k.wait_all("sp", [r_kcs])
            k.barrier()
        stAB.close()

        stC = ExitStack()
        with stC:
            USJs, r_USJs = SB(stC, "USJs", [128, 32, 16], BF16)
            k.op("pool", lambda e: e.memset(USJs[:, :, :], 0.0), writes=[r_USJs])
            r_USJsl = [Res() for _ in range(4)]
            Uv = Ud.rearrange("c (g j) s n -> s j (c g) n", j=16)
            Usv = Usd.rearrange("c (g j) t b -> t j (c g) b", j=16)
            for t in range(4):
                k.dma("sp", USJs[64 + 16 * t:64 + 16 * t + 16, :, :], Usv[t], reads=r_Usdl + [r_USJs], writes=[r_USJsl[t]])
            iot, r_iot = ld(stC, "iot", iota_d, [128, NT])
            YSC, r_YSC = SB(stC, "YSC", [128, 32, NMAIN], BF16)
            Hend, r_Hend = SB(stC, "Hend", [128, 32])
            NB = 2
            W = {}
            for nm in ("SA", "SB", "cs", "sn", "ab", "tq", "fr", "m1", "m2", "S1", "S2", "G1", "G2", "h1", "h2"):
                W[nm] = [SB(stC, "w_%s%d" % (nm, i), [128, NT]) for i in range(NB)]
            Wti = [SB(stC, "w_ti%d" % i, [128, NT], I32) for i in range(NB)]
            Hb = [SB(stC, "Hb%d" % i, [128, NMAIN], BF16) for i in range(NB)]
            pSA = [PS(stC, "pSA%d" % i, [128, 2, 512]) for i in range(1)]
            pSB = [PS(stC, "pSB%d" % i, [128, 2, 512]) for i in range(1)]
            pY = [PS(stC, "pY%d" % i, [128, NMAIN]) for i in range(2)]
            HALF = NT // 2
            NO = NMAIN + 1
            r_YSCb = [Res() for _ in range(4)]
            r_Ydb = [[Res() for _ in range(8)] for _ in range(4)]

            h0s, r_h0s = ld(stC, "h0s", h0s_d, [128, 32, 16])
            h0d, r_h0d = ld(stC, "h0d", h0d_d, [128, 32, 16])
            h0b, r_h0b = SB(stC, "h0b", [128, 32, 16], BF16)
            k.op("pool", lambda e: e.tensor_copy(h0b[:, :, :], h0s[:, :, :]), reads=[r_h0s], writes=[r_h0b])
            pys, r_pys = PS(stC, "pys", [128, 512]); pss, r_pss = PS(stC, "pss", [128, 512])
            pysv = pys[0:64, :].rearrange("p (g b) -> p g b", b=16)
            pssv = pss[:, :].rearrange("p (g b) -> p g b", b=16)
            r_G2h = [(Res(), Res()) for _ in range(NB)]
            sgn2pi, r_sgn = SB(stC, "sgn2pi", [128, 1])
            k.op("pool", lambda e: e.memset(sgn2pi[0:64, :], -TWO_PI_S), writes=[r_sgn])
            k.op("pool", lambda e: e.memset(sgn2pi[64:128, :], TWO_PI_S), writes=[r_sgn])

            def ssm_group(g):
                i2 = g % NB
                (sa, r_sa), (sb_, r_sb) = W["SA"][i2], W["SB"][i2]
                (tq, r_tq), (ti_, r_ti), (fr, r_fr), (ab, r_ab) = W["tq"][i2], Wti[i2], W["fr"][i2], W["ab"][i2]
                (cs, r_cs), (sn, r_sn) = W["cs"][i2], W["sn"][i2]
                k.op("act", lambda e: e.activation(tq[:, :], iot[:, :], AF.Copy, scale=TH8[:, g:g + 1]),
                     reads=[r_iot, r_TH8], writes=[r_tq])
                (psa, r_psa), (psb, r_psb) = pSA[0], pSB[0]
                for h2 in range(2):
                    k.op("pe", lambda e: e.matmul(psa[:, h2, 0:HALF], Pbf[:, g, :], USJ[:, g, h2 * HALF:(h2 + 1) * HALF], start=True, stop=True),
                         reads=[r_Pbf] + r_USJl, writes=[r_psa])
                    k.op("pe", lambda e: e.matmul(psb[:, h2, 0:HALF], Pdbf[:, g, :], USJ[:, g, h2 * HALF:(h2 + 1) * HALF], start=True, stop=True),
                         reads=[r_Pdbf] + r_USJl, writes=[r_psb])
                k.op("act", lambda e: e.copy(sa[:, :].rearrange("p (a n) -> p a n", a=2), psa[:, :, 0:HALF]), reads=[r_psa], writes=[r_sa])
                k.op("act", lambda e: e.copy(sb_[:, :].rearrange("p (a n) -> p a n", a=2), psb[:, :, 0:HALF]), reads=[r_psb], writes=[r_sb])
                yield
                k.op("dve", lambda e: e.tensor_copy(ti_[:, :], tq[:, :]), reads=[r_tq], writes=[r_ti])
                k.op("dve", lambda e: e.tensor_tensor(fr[:, :], tq[:, :], ti_[:, :], ALU.subtract), reads=[r_tq, r_ti], writes=[r_fr])
                k.op("act", lambda e: e.activation(sn[:, :], fr[:, :], AF.Sin, scale=TWO_PI_S), reads=[r_fr], writes=[r_sn])
                k.op("act", lambda e: e.activation(ab[:, :], fr[:, :], AF.Abs), reads=[r_fr], writes=[r_ab])
                k.op("act", lambda e: e.activation(cs[:, :], ab[:, :], AF.Sin, scale=-TWO_PI_S, bias=hpi[:, :]),
                     reads=[r_ab, r_hpi], writes=[r_cs])
                (m1, r_m1_), (m2, r_m2_), (S1, r_S1), (S2, r_S2) = W["m1"][i2], W["m2"][i2], W["S1"][i2], W["S2"][i2]
                k.op("act", lambda e: e.activation(m2[:, NPRE - 1:NT], fr[:, NPRE - 1:NT], AF.Sin, scale=sgn2pi[:, 0:1]),
                     reads=[r_fr, r_sgn], writes=[r_m2_])
                (G1, r_G1), (G2, r_G2) = W["G1"][i2], W["G2"][i2]
                yield
                k.op("dve", lambda e: e.tensor_tensor(m1[:, :], sb_[:, :], sn[:, :], ALU.mult), reads=[r_sb, r_sn], writes=[r_m1_])
                k.op("dve", lambda e: e.tensor_tensor(S1[:, :], sa[:, :], cs[:, :], ALU.mult), reads=[r_sa, r_cs], writes=[r_S1])
                k.op("dve", lambda e: e.tensor_tensor(S1[:, :], S1[:, :], m1[:, :], ALU.add), reads=[r_S1, r_m1_], writes=[r_S1])
                yield
                r8b = R8[:, g:g + 1].to_broadcast([128, NT])
                osl = slice(NPRE - 1, NT)
                k.op("dve", lambda e: e.tensor_tensor_scan(G1[:, :], r8b, S1[:, :], 0.0, ALU.mult, ALU.add),
                     reads=[r_R8, r_S1], writes=[r_G1])
                (r_g2lo, r_g2hi) = r_G2h[i2]
                k.dma("sp", G2[0:64, osl], G1[64:128, osl], reads=[r_G1], writes=[r_g2lo])
                k.dma("sp", G2[64:128, osl], G1[0:64, osl], reads=[r_G1], writes=[r_g2hi])
                yield
                (h1, r_h1), (h2_, r_h2) = W["h1"][i2], W["h2"][i2]
                k.op("dve", lambda e: e.tensor_tensor(h1[:, osl], G1[:, osl], cs[:, osl], ALU.mult), reads=[r_G1, r_cs], writes=[r_h1])
                k.op("dve", lambda e: e.tensor_tensor(h2_[:, osl], G2[:, osl], m2[:, osl], ALU.mult), reads=[r_g2lo, r_g2hi, r_m2_], writes=[r_h2])
                (hb, r_hb) = Hb[i2]
                k.op("dve", lambda e: e.tensor_tensor(hb[:, :], h1[:, NPRE - 1:NT - 1], h2_[:, NPRE - 1:NT - 1], ALU.add),
                     reads=[r_h1, r_h2], writes=[r_hb])
                k.op("pool", lambda e: e.tensor_tensor(Hend[:, g:g + 1], h1[:, NT - 1:NT], h2_[:, NT - 1:NT], ALU.add),
                     reads=[r_h1, r_h2], writes=[r_Hend])
                (py, r_py) = pY[g % 2]
                k.op("pe", lambda e: e.matmul(py[:, :], Mbf[:, g, :], USJ[:, g, NPRE:NT], start=True, stop=False),
                     reads=[r_Mbf] + r_USJl, writes=[r_py])
                k.op("pe", lambda e: e.matmul(py[:, :], Qbf[:, g, :], hb[:, :], start=False, stop=True),
                     reads=[r_Qbf, r_hb], writes=[r_py])
                k.op("act", lambda e: e.copy(YSC[:, g, :], py[:, :]), reads=[r_py], writes=[r_YSCb[g // 8]])
                k.op("pe", lambda e: e.matmul(pysv[:, g, :], Mbf[:, g, 64:128], USJs[:, g, :], start=True, stop=False),
                     reads=[r_Mbf, r_USJs] + r_USJsl, writes=[r_pys])
                k.op("pe", lambda e: e.matmul(pysv[:, g, :], Qbf[:, g, 0:64], h0b[:, g, :], start=False, stop=True),
                     reads=[r_Qbf, r_h0b], writes=[r_pys])
                k.op("pe", lambda e: e.matmul(pssv[:, g, :], Pbf[:, g, :], USJs[:, g, :], start=True, stop=True),
                     reads=[r_Pbf, r_USJs] + r_USJsl, writes=[r_pss])
                if g % 8 == 7:
                    gb0 = g - 7
                    for s_ in range(8):
                        k.dma("sp", Yd[s_][gb0 * 16:(gb0 + 8) * 16, :].rearrange("(g c) n -> c g n", c=16),
                              YSC[16 * s_:16 * s_ + 16, gb0:gb0 + 8, :], reads=[r_YSCb[g // 8]], writes=[r_Ydb[g // 8][s_]])
            sg = [ssm_group(g) for g in range(32)]

            def seg(i):
                try:
                    next(sg[i])
                except StopIteration:
                    pass

            seg(0); seg(0)
            for g in range(32):
                if g + 1 < 32:
                    seg(g + 1)
                seg(g)
                if g >= 1:
                    seg(g - 1); seg(g - 1)
                seg(g)
                if g + 1 < 32:
                    seg(g + 1)
            seg(31); seg(31)
            k.dma("sp", hend_o[:, :], Hend[:, :], reads=[r_Hend])
            YSs, r_YSs = SB(stC, "YSs", [64, 32, 16], BF16)
            k.op("act", lambda e: e.copy(YSs[:, :, :], pysv), reads=[r_pys], writes=[r_YSs])
            hn1, r_hn1 = SB(stC, "hn1", [128, 32, 16]); hn2, r_hn2 = SB(stC, "hn2", [128, 32, 16])
            s5 = [128, 32, 16]
            k.op("dve", lambda e: e.tensor_tensor(hn1[:, :, :], h0s[:, :, :], bc(E4r[:, :], 2, s5), ALU.mult), reads=[r_h0s, r_E4], writes=[r_hn1])
            k.op("pool", lambda e: e.tensor_tensor(hn2[:, :, :], h0d[:, :, :], bc(E4i[:, :], 2, s5), ALU.mult), reads=[r_h0d, r_E4], writes=[r_hn2])
            k.op("dve", lambda e: e.tensor_tensor(hn1[:, :, :], hn1[:, :, :], hn2[:, :, :], ALU.add), reads=[r_hn1, r_hn2], writes=[r_hn1])
            k.op("dve", lambda e: e.tensor_tensor(hn1[:, :, :], hn1[:, :, :], pssv, ALU.add), reads=[r_hn1, r_pss], writes=[r_hn1])
            k.dma("sp", hs_new[:, :, :], hn1[:, :, :], reads=[r_hn1])
            r_Yd = Res(); r_Ysd = Res()
            r_Ysds = [Res() for _ in range(4)]
            for t_ in range(4):
                k.dma("sp", Ysd[t_].rearrange("(g c) b -> c g b", c=16), YSs[16 * t_:16 * t_ + 16, :, :], reads=[r_YSs], writes=[r_Ysds[t_]])
            k.wait_all("sp", [r_Hend, r_hn1])
            k.barrier()
        stAC.close()

        stC2 = ExitStack()
        with stC2:
            YT, r_YT = SB(stC2, "YT", [128, 4, NCOL], BF16)
            UTm, r_UTm = SB(stC2, "UTm", [128, 4, NCOL], BF16)
            r_YTl = [Res() for _ in range(8)]; r_UTml = [Res() for _ in range(8)]
            for ch in range(4):
                k.dma("sp", YT[:, ch, 0:2176].rearrange("p (s n) -> p s n", s=8),
                      Yd[:, ch * 128:(ch + 1) * 128, :].rearrange("s p n -> p s n"), reads=r_Ydb[ch], writes=[r_YTl[ch]])
                k.dma("sp", YT[:, ch, 2176:2240].rearrange("p (t b) -> p t b", t=4),
                      Ysd[:, ch * 128:(ch + 1) * 128, :].rearrange("t p b -> p t b"), reads=r_Ysds, writes=[r_YTl[4 + ch]])
            for ch in range(4):
                k.dma("sp", UTm[:, ch, 0:2176].rearrange("p (s n) -> p s n", s=8), Ud[ch][:, :, NPRE:NT],
                      reads=[r_Udl[ch]], writes=[r_UTml[ch]])
                k.dma("sp", UTm[:, ch, 2176:2240].rearrange("p (t b) -> p t b", t=4), Usd[ch], reads=[r_Usdl[ch]], writes=[r_UTml[4 + ch]])
            dvec, r_dvec = ld(stC2, "dvec", dvec_d, [128, 4])
            wglu, r_wglu = SB(stC2, "wglu", [128, 4, 512], BF16)
            k.dma("pool", wglu[:, :, :], w_glu_d.rearrange("(c p) n -> p c n", p=128), writes=[r_wglu])
            z, r_z = SB(stC2, "z", [128, 4, NCOL])
            zb, r_zb = SB(stC2, "zb", [128, 4, NCOL], BF16)
            HC = NCOL // 2
            yfl = [SB(stC2, "yf%d" % i, [128, HC]) for i in range(3)]
            ttl = [SB(stC2, "tt%d" % i, [128, HC]) for i in range(3)]

            def gelu_unit(u):
                ch, hh = u // 2, u % 2
                cs_ = slice(hh * HC, (hh + 1) * HC)
                (yf, r_yf), (tt, r_tt) = yfl[u % 3], ttl[u % 3]
                k.op("dve", lambda e: e.scalar_tensor_tensor(yf[:, :], UTm[:, ch, cs_], dvec[:, ch:ch + 1], YT[:, ch, cs_], ALU.mult, ALU.add),
                     reads=[r_UTml[ch], r_UTml[4 + ch], r_dvec, r_YTl[ch], r_YTl[4 + ch]], writes=[r_yf])
                k.op("act", lambda e: e.activation(tt[:, :], yf[:, :], AF.Square), reads=[r_yf], writes=[r_tt])
                yield
                k.op("dve", lambda e: e.tensor_scalar(tt[:, :], tt[:, :], 0.044715, 1.0, ALU.mult, ALU.add), reads=[r_tt], writes=[r_tt])
                k.op("dve", lambda e: e.tensor_tensor(tt[:, :], tt[:, :], yf[:, :], ALU.mult), reads=[r_tt, r_yf], writes=[r_tt])
                k.op("act", lambda e: e.activation(tt[:, :], tt[:, :], AF.Sigmoid, scale=2.0 * math.sqrt(2.0 / math.pi)),
                     reads=[r_tt], writes=[r_tt])
                yield
                k.op("dve", lambda e: e.tensor_tensor(z[:, ch, cs_], yf[:, :], tt[:, :], ALU.mult), reads=[r_yf, r_tt], writes=[r_z])
                k.op("act", lambda e: e.copy(zb[:, ch, cs_], z[:, ch, cs_]), reads=[r_z], writes=[r_zb])

            run_staged([gelu_unit(u) for u in range(8)], 3, oldest_first=True)
            pg = [PS(stC2, "pg%d" % i, [128, 512]) for i in range(3)]
            pss2, r_pss2 = PS(stC2, "pss2", [128, 512])
            sg = [SB(stC2, "sg%d" % i, [128, 512]) for i in range(2)]
            sq = [SB(stC2, "sq%d" % i, [128, 512], BF16) for i in range(2)]
            rstd, r_rstd = SB(stC2, "rstd", [128, NCOL])
            cblocks = [(c0, min(512, NCOL - c0)) for c0 in range(0, NCOL, 512)]
            def gate_unit(cnt, c0, cw, ec):
                csl = slice(c0, c0 + cw)
                (pg_, r_pg) = pg[cnt % 3]; (sg_, r_sg) = sg[cnt % 2]; (sq_, r_sq) = sq[cnt % 2]
                for cc in range(4):
                    k.op("pe", lambda e: e.matmul(pg_[:, 0:cw], wglu[:, cc, ec * 128:(ec + 1) * 128], zb[:, cc, csl],
                                                  start=(cc == 0), stop=(cc == 3)),
                         reads=[r_wglu, r_zb], writes=[r_pg], sig=SG("C2_g", cc == 3), chain=CH("C2_g", cc > 0))
                yield
                k.op("act", lambda e: e.activation(sg_[:, 0:cw], pg_[:, 0:cw], AF.Sigmoid), reads=[r_pg], writes=[r_sg])
                yield
                k.op("dve", lambda e: e.tensor_tensor(z[:, ec, csl], z[:, ec, csl], sg_[:, 0:cw], ALU.mult),
                     reads=[r_z, r_sg, r_zb], writes=[r_z])
                k.op("act", lambda e: e.activation(sq_[:, 0:cw], z[:, ec, csl], AF.Square),
                     reads=[r_z], writes=[r_sq])
                k.op("pe", lambda e: e.matmul(pss2[:, 0:cw], onesb[:, :], sq_[:, 0:cw], start=(ec == 0), stop=(ec == 3)),
                     reads=[r_onesb, r_sq], writes=[r_pss2])
                if ec == 3:
                    k.op("act", lambda e: e.activation(rstd[:, csl], pss2[:, 0:cw], AF.Sqrt, bias=EPS, scale=1.0 / 512),
                         reads=[r_pss2], writes=[r_rstd])

            units = []
            for (c0_, cw_) in cblocks:
                for ec in range(4):
                    units.append(gate_unit(len(units), c0_, cw_, ec))
            def useg(i):
                try:
                    next(units[i])
                except StopIteration:
                    pass

            useg(0); useg(0)
            for i in range(len(units)):
                if i + 1 < len(units):
                    useg(i + 1)
                useg(i)
                if i + 1 < len(units):
                    useg(i + 1)
            k.op("dve", lambda e: e.reciprocal(rstd[:, :], rstd[:, :]), reads=[r_rstd], writes=[r_rstd])
            for ec in range(4):
                k.op("dve",
                     lambda e: e.tensor_tensor(catT[:, 4 + ec, 0:2176].rearrange("p (n s) -> p s n", s=8),
                                               z[:, ec, 0:2176].rearrange("p (s n) -> p s n", s=8),
                                               rstd[:, 0:2176].rearrange("p (s n) -> p s n", s=8), ALU.mult),
                     reads=[r_z, r_rstd], writes=[r_catT])
                k.op("dve",
                     lambda e: e.tensor_tensor(catT[:, 4 + ec, 2176:2240].rearrange("p (b t) -> p t b", t=4),
                                               z[:, ec, 2176:2240].rearrange("p (t b) -> p t b", t=4),
                                               rstd[:, 2176:2240].rearrange("p (t b) -> p t b", t=4), ALU.mult),
                     reads=[r_z, r_rstd], writes=[r_catT])
            k.barrier()

        stD = ExitStack()
        with stD:
            wup, r_wup = SB(stD, "wup", [128, 8, 4096], BF16)
            wdn, r_wdn = SB(stD, "wdn", [128, 32, 1024], BF16)
            gmlp, r_gmlp = ld(stD, "gmlp", gmlp_d, [128, 8])
            r_cat = [Res() for _ in range(18)]
            r_y = [Res() for _ in range(18)]
            dtiles = [(i, 128) for i in range(1, 17)] + [(17, 64)]

            def yrows(ti):
                return y_main[(ti - 1) * 128: ti * 128, :] if ti < 17 else y_s[:, :]

            stD1 = ExitStack()
            with stD1:
                wout, r_wout = SB(stD1, "wout", [128, 8, 1024], BF16)
                k.dma("pool", wout[:, :, :], w_out_d.rearrange("(c p) n -> p c n", p=128), writes=[r_wout])
                gcat, r_gcat = ld(stD1, "gcat", gcat_d, [128, 8])
                gpost, r_gpost = ld(stD1, "gpost", gpost_d, [128, 1024], BF16, q="pool")
                r_wupc = [Res() for _ in range(8)]
                for c in range(8):
                    k.dma("pool", wup[:, c, :], w_up_d[c * 128:(c + 1) * 128, :], writes=[r_wupc[c]])
                r_wdnc = [Res() for _ in range(4)]
                for c in range(4):
                    k.dma("pool", wdn[:, c * 8:(c + 1) * 8, :], w_down_d[c * 1024:(c + 1) * 1024, :].rearrange("(c p) n -> p c n", p=128),
                          writes=[r_wdnc[c]])
                for c in range(8):
                    k.op("dve", lambda e: e.tensor_scalar(wout[:, c, :], wout[:, c, :], gcat[:, c:c + 1], None, ALU.mult),
                         reads=[r_gcat, r_wout], writes=[r_wout])
                xr = [SB(stD1, "xr%d" % i, [128, 1024]) for i in range(2)]
                h1t = [SB(stD1, "h1t%d" % i, [128, 1024]) for i in range(2)]
                xn2 = [SB(stD1, "xn2%d" % i, [128, 1024], BF16) for i in range(2)]
                sdl = [SB(stD1, "sd%d" % i, [128, 1]) for i in range(2)]
                sd2l = [SB(stD1, "sd2%d" % i, [128, 1]) for i in range(2)]
                junkd, r_junkd = SB(stD1, "junkd", [128, 1024], BF16)
                pmixl = [PS(stD1, "pmix%d" % i, [128, 2, 512]) for i in range(2)]
                ptpl = [PS(stD1, "ptp%d" % i, [128, 8, 128], BF16) for i in range(2)]
                def d1_body(it, ti, np_):
                    PSL = slice(0, np_)
                    b2 = it % 2
                    (xr_, r_xr), (h1_, r_h1), (xn_, r_xn), (sd, r_sd), (sd2, r_sd2) = xr[b2], h1t[b2], xn2[b2], sdl[b2], sd2l[b2]
                    (pmix, r_pmix), (ptp, r_ptp) = pmixl[b2], ptpl[b2]
                    col0 = ti * 128
                    k.dma("sp", xr_[PSL, :], (xm[ti * 128: ti * 128 + 128, :] if ti < 17 else xs[:, :]), writes=[r_xr])
                    for hf in range(2):
                        for e_ in range(8):
                            k.op("pe", lambda e: e.matmul(pmix[PSL, hf, :], catT[:, e_, col0:col0 + np_], wout[:, e_, hf * 512:(hf + 1) * 512],
                                                          start=(e_ == 0), stop=(e_ == 7)),
                                 reads=[r_cat[ti], r_wout], writes=[r_pmix], sig=SG("D_o", e_ == 7 and hf == 1), chain=CH("D_o", e_ > 0))
                    yield
                    pmv = pmix[PSL, :, :].rearrange("p a n -> p (a n)")
                    k.op("act", lambda e: e.activation(junkd[PSL, :], pmv, AF.Square, accum_out=sd[PSL, :]),
                         reads=[r_pmix], writes=[r_junkd, r_sd])
                    k.op("act", lambda e: e.activation(sd[PSL, :], sd[PSL, :], AF.Sqrt, bias=EPS, scale=1.0 / 1024), reads=[r_sd], writes=[r_sd])
                    k.op("dve", lambda e: e.reciprocal(sd[PSL, :], sd[PSL, :]), reads=[r_sd], writes=[r_sd])
                    k.op("dve", lambda e: e.scalar_tensor_tensor(h1_[PSL, :], pmv, sd[PSL, :], gpost[PSL, :], ALU.mult, ALU.mult),
                         reads=[r_pmix, r_sd, r_gpost], writes=[r_h1])
                    k.op("dve", lambda e: e.tensor_tensor(h1_[PSL, :], h1_[PSL, :], xr_[PSL, :], ALU.add), reads=[r_h1, r_xr], writes=[r_h1])
                    k.dma("sp", yrows(ti), h1_[PSL, :], reads=[r_h1], writes=[r_y[ti]])
                    yield
                    k.op("act", lambda e: e.activation(junkd[PSL, :], h1_[PSL, :], AF.Square, accum_out=sd2[PSL, :]),
                         reads=[r_h1], writes=[r_junkd, r_sd2])
                    k.op("act", lambda e: e.activation(sd2[PSL, :], sd2[PSL, :], AF.Sqrt, bias=EPS, scale=1.0 / 1024), reads=[r_sd2], writes=[r_sd2])
                    k.op("dve", lambda e: e.reciprocal(sd2[PSL, :], sd2[PSL, :]), reads=[r_sd2], writes=[r_sd2])
                    k.op("act", lambda e: e.activation(xn_[PSL, :], h1_[PSL, :], AF.Copy, scale=sd2[PSL, :]), reads=[r_h1, r_sd2], writes=[r_xn])
                    yield
                    for c in range(8):
                        k.op("pe", lambda e: e.transpose(ptp[:, c, PSL], xn_[PSL, c * 128:(c + 1) * 128], ident[PSL, PSL]),
                             reads=[r_xn, r_ident], writes=[r_ptp], sig=SG("D_tp", c == 7))
                    k.op("dve", lambda e: e.tensor_copy(catT[:, :, col0:col0 + np_], ptp[:, :, PSL]), reads=[r_ptp], writes=[r_cat[ti]])
                run_staged([d1_body(it, ti, np_) for it, (ti, np_) in enumerate(dtiles)], 4)
                k.barrier()

            stD2 = ExitStack()
            with stD2:
                for c in range(8):
                    k.op("dve", lambda e: e.tensor_scalar(wup[:, c, :], wup[:, c, :], gmlp[:, c:c + 1], None, ALU.mult),
                         reads=[r_gmlp], writes=[r_wupc[c]])
                gmpost, r_gmpost = ld(stD2, "gmpost", gmpost_d, [128, 1024], BF16, q="pool")
                h1r = [SB(stD2, "h1r%d" % i, [128, 1024]) for i in range(2)]
                tmpl = [SB(stD2, "tmp%d" % i, [128, 1024]) for i in range(2)]
                junk2, r_junk2 = SB(stD2, "junk2", [128, 1024], BF16)
                aTw = [SB(stD2, "aTw%d" % i, [128, 2, 256], BF16) for i in range(3)]
                rl = [SB(stD2, "rl%d" % i, [128, 2, 256], BF16) for i in range(2)]
                sd3l = [SB(stD2, "sd3%d" % i, [128, 1]) for i in range(2)]
                pdn = [PS(stD2, "pdn%d" % i, [128, 2, 512]) for i in range(2)]
                pup = [PS(stD2, "pup%d" % i, [128, 2, 256]) for i in range(2)]
                groups = [[(2 * g + 1, 128), (2 * g + 2, 128)] for g in range(8)]
                groups.append([(17, 64)])
                NF2 = 16
                for gi, tiles in enumerate(groups):
                    ncols = sum(np_ for _, np_ in tiles)
                    c0 = tiles[0][0] * 128
                    CS = slice(c0, c0 + ncols)
                    rcs = [r_cat[ti] for ti, _ in tiles]
                    for li, (ti, np_) in enumerate(tiles):
                        (h1_, r_h1) = h1r[li]
                        k.dma("sp", h1_[0:np_, :], yrows(ti), reads=[r_y[ti]], writes=[r_h1])

                    def up(f2):
                        (pu_, r_pu_) = pup[f2 % 2]; (rl_, r_rl) = rl[f2 % 2]; (aw, r_aw) = aTw[f2 % 3]
                        for fi in range(2):
                            fc = f2 * 2 + fi
                            for c in range(8):
                                k.op("pe", lambda e: e.matmul(pu_[:, fi, 0:ncols], wup[:, c, fc * 128:(fc + 1) * 128], catT[:, c, CS],
                                                              start=(c == 0), stop=(c == 7)),
                                     reads=[r_wupc[c]] + rcs, writes=[r_pu_], sig=SG("D_up", c == 7 and fi == 1), chain=CH("D_up", c > 0))
                        k.op("act", lambda e: e.activation(rl_[:, :, 0:ncols], pu_[:, :, 0:ncols], AF.Relu), reads=[r_pu_], writes=[r_rl])
                        k.op("dve", lambda e: e.tensor_tensor(aw[:, :, 0:ncols], rl_[:, :, 0:ncols], rl_[:, :, 0:ncols], ALU.mult),
                             reads=[r_rl], writes=[r_aw])

                    def down(f2):
                        (aw, r_aw) = aTw[f2 % 3]
                        to = 0
                        for li, (ti, np_) in enumerate(tiles):
                            (pd_, r_pd) = pdn[li]
                            for hf in range(2):
                                for fi in range(2):
                                    fc = f2 * 2 + fi
                                    k.op("pe", lambda e: e.matmul(pd_[0:np_, hf, :], aw[:, fi, to:to + np_], wdn[:, fc, hf * 512:(hf + 1) * 512],
                                                                  start=(fc == 0), stop=(fc == 2 * NF2 - 1)),
                                         reads=[r_aw, r_wdnc[fc // 8]], writes=[r_pd], sig=SG("D_dn", hf == 1 and fi == 1), chain=CH("D_dn", fc > 0))
                            to += np_

                    up(0)
                    for f2 in range(NF2):
                        if f2 + 1 < NF2:
                            up(f2 + 1)
                        down(f2)
                    for li, (ti, np_) in enumerate(tiles):
                        PSL = slice(0, np_)
                        (pd_, r_pd) = pdn[li]; (h1_, r_h1) = h1r[li]; (tmp, r_tmp) = tmpl[li]; (sd3, r_sd3) = sd3l[li]
                        pdv = pd_[PSL, :, :].rearrange("p a n -> p (a n)")
                        k.op("act", lambda e: e.activation(junk2[PSL, :], pdv, AF.Square, accum_out=sd3[PSL, :]),
                             reads=[r_pd], writes=[r_junk2, r_sd3])
                        k.op("act", lambda e: e.activation(sd3[PSL, :], sd3[PSL, :], AF.Sqrt, bias=EPS, scale=1.0 / 1024), reads=[r_sd3], writes=[r_sd3])
                        k.op("dve", lambda e: e.reciprocal(sd3[PSL, :], sd3[PSL, :]), reads=[r_sd3], writes=[r_sd3])
                        k.op("dve", lambda e: e.scalar_tensor_tensor(tmp[PSL, :], pdv, sd3[PSL, :], gmpost[PSL, :], ALU.mult, ALU.mult),
                             reads=[r_pd, r_sd3, r_gmpost], writes=[r_tmp])
                        k.op("pool", lambda e: e.tensor_tensor(tmp[PSL, :], tmp[PSL, :], h1_[PSL, :], ALU.add), reads=[r_tmp, r_h1], writes=[r_tmp])
                        k.dma("sp", yrows(ti), tmp[PSL, :], reads=[r_tmp], writes=[r_y[ti]])
                k.wait_all("sp", r_y)
                k.barrier()
        print("n_inst", k.n_inst)
    return nc


_NC_CACHE = {}


def _host_inputs(inp):
    f = np.float32
    x_prompt = np.asarray(inp["x_prompt"], f); x_sample = np.asarray(inp["x_sample"], f)
    meta = np.asarray(inp["meta_tokens"], f)
    w_in = np.asarray(inp["w_in"], f)[0]
    w_in_ext = np.concatenate([w_in[:, 0:512], w_in[:, 512:576], w_in[:, 512:576], w_in[:, 576:640], w_in[:, 576:640],
                               w_in[:, 640:768], w_in[:, 768:1280]], axis=1)

    def pc(v):
        return np.ascontiguousarray(np.asarray(v, f).reshape(8, 128).T)

    gcat = np.concatenate([np.asarray(inp["norm_att_out"], f)[0], np.asarray(inp["norm_ssm_out"], f)[0]])
    a_re = np.asarray(inp["ssm_a_re"], f)[0]; a_im = np.asarray(inp["ssm_a_im"], f)[0]
    dup = lambda a: np.ascontiguousarray(np.concatenate([a, a], axis=0))
    b_re = np.asarray(inp["ssm_b_re"], f)[0]; b_im = np.asarray(inp["ssm_b_im"], f)[0]
    c_re = np.asarray(inp["ssm_c_re"], f)[0]; c_im = np.asarray(inp["ssm_c_im"], f)[0]
    half = 8
    invf = (500000.0 ** (-np.arange(half, dtype=np.float64) / half) / (2 * np.pi)).astype(f)
    s_i = np.arange(128)[:, None]; q_i = np.arange(128)[None, :]
    maskp = (s_i > q_i).astype(f); maskc = (s_i <= q_i).astype(f)
    ks = np.arange(64)[:, None]; qs = np.arange(64)[None, :]
    masks = np.zeros((128, 64), f)
    masks[:64] = ((ks // 4 == qs // 4) & (ks % 4 <= qs % 4)).astype(f)
    sj = np.arange(128)[:, None] // 16; tc = np.arange(128)[None, :] // 16
    maskM = (tc >= sj).astype(f)
    common = {
        "invf": np.broadcast_to(invf, (128, 8)).copy(),
        "maskp": maskp, "maskc": maskc, "masks": masks, "maskM": maskM,
        "iota_n": np.broadcast_to(np.arange(NT, dtype=f), (128, NT)).copy(),
        "kvals": np.broadcast_to(np.arange(-7, 9, dtype=f), (128, 16)).copy(),
        "kvals_r": np.broadcast_to(np.arange(8, -8, -1, dtype=f), (128, 16)).copy(),
        "w_in": np.ascontiguousarray(w_in_ext), "w_out": np.asarray(inp["w_out"], f)[0],
        "w_up": np.asarray(inp["w_up"], f)[0], "w_down": np.asarray(inp["w_down"], f)[0],
        "w_glu": np.asarray(inp["w_glu"], f)[0],
        "gpre": pc(inp["norm_mix_pre"][0]), "gcat": pc(gcat), "gmlp": pc(inp["norm_mlp_pre"][0]),
        "gpost": np.broadcast_to(np.asarray(inp["norm_mix_post"], f)[0], (128, 1024)).copy(),
        "gmpost": np.broadcast_to(np.asarray(inp["norm_mlp_post"], f)[0], (128, 1024)).copy(),
        "sinks": np.broadcast_to(np.asarray(inp["attn_sinks"], f)[0], (128, 8)).copy(),
        "dvec": np.ascontiguousarray(np.asarray(inp["ssm_d"], f)[0].reshape(4, 128).T),
        "art": dup(a_re.T), "ait": dup(a_im.T),
        "ldt": np.broadcast_to(np.asarray(inp["ssm_log_dt"], f)[0], (128, 32)).copy(),
        "btr": dup(b_re.transpose(1, 0, 2)), "bti": dup(b_im.transpose(1, 0, 2)),
        "ctr": dup(c_re.transpose(2, 0, 1)), "cti": dup(c_im.transpose(2, 0, 1)),
    }
    ck = np.asarray(inp["cache_k_win"], f)[0].reshape(128, 128, 128)
    cv = np.asarray(inp["cache_v_win"], f)[0].reshape(128, 128, 128)
    sre = np.asarray(inp["state_ssm_re"], f)[0]; sim = np.asarray(inp["state_ssm_im"], f)[0]
    maps = []
    for c in range(8):
        b, hf = c // 2, c % 2
        m = dict(common)
        xm = np.zeros((2176, 1024), f); xp = np.zeros((2048, 1024), f)
        pos = np.zeros((128, 18), f)
        p_i = np.arange(128)
        if hf == 0:
            xm[112:128] = meta; xm[128:] = x_prompt[b, 0:2048]
            for i in range(17):
                pos[:, i] = np.maximum(i * 128 + p_i - 112, 0)
            mask1 = maskp * (s_i >= 112)
        else:
            xm[:] = x_prompt[b, 1920:4096]
            xp[112:128] = meta; xp[128:] = x_prompt[b, 0:1920]
            for i in range(17):
                pos[:, i] = 1936 + i * 128 + p_i
            mask1 = maskp
        pos[:, 17] = 8192 + (p_i % 4)
        m["xm"] = xm; m["xp"] = xp
        m["xs"] = np.ascontiguousarray(x_sample[16 * c:16 * c + 16].reshape(64, 1024))
        m["pos"] = pos; m["mask1"] = np.ascontiguousarray(mask1.astype(f))
        ckc = ck[16 * c:16 * c + 16]; cvc = cv[16 * c:16 * c + 16]
        m["ck"] = np.ascontiguousarray(ckc); m["cv"] = np.ascontiguousarray(cvc)
        m["ckd"] = np.ascontiguousarray(np.concatenate([ckc[:, :, 0:64], ckc[:, :, 0:64], ckc[:, :, 64:128], ckc[:, :, 64:128]], axis=2))
        r_ = sre[16 * c:16 * c + 16].transpose(2, 1, 0); i_ = sim[16 * c:16 * c + 16].transpose(2, 1, 0)
        m["h0s"] = np.ascontiguousarray(np.concatenate([r_, i_], axis=0))
        m["h0d"] = np.ascontiguousarray(np.concatenate([i_, r_], axis=0))
        maps.append(m)
    return maps


def kernel(**inputs):
    if "nc" not in _NC_CACHE:
        _NC_CACHE["nc"] = build_program()
    nc = _NC_CACHE["nc"]
    maps = _host_inputs(inputs)
    res = run_bass_kernel_spmd(nc, maps, core_ids=list(range(8)))
    R = res.results
    f = np.float32
    y_prompt = np.zeros((4, 4096, 1024), f)
    kwp = np.zeros((1, 4, 128, 2, 64), f); vwp = np.zeros((1, 4, 128, 2, 64), f)
    srp = np.zeros((1, 4, 32, 64), f); sip = np.zeros((1, 4, 32, 64), f)
    y_sample = np.zeros((128, 4, 1024), f)
    kws = np.zeros((1, 128, 128, 2, 64), f); vws = np.zeros((1, 128, 128, 2, 64), f)
    srs = np.zeros((1, 128, 32, 64), f); sis = np.zeros((1, 128, 32, 64), f)
    for c in range(8):
        b, hf = c // 2, c % 2
        r = R[c]
        ym = np.asarray(r["y_main"], f)
        if hf == 0:
            y_prompt[b, 0:2048] = ym
        else:
            y_prompt[b, 2048:4096] = ym
            kwp[0, b] = np.asarray(r["kwin"], f).reshape(128, 2, 64)
            vwp[0, b] = np.asarray(r["vwin"], f).reshape(128, 2, 64)
            he = np.asarray(r["hend"], f)
            srp[0, b] = he[0:64].T; sip[0, b] = he[64:128].T
        sl = slice(16 * c, 16 * c + 16)
        y_sample[sl] = np.asarray(r["y_s"], f).reshape(16, 4, 1024)
        kws[0, sl, 0:124] = np.asarray(r["kc_s"], f).reshape(16, 124, 2, 64)
        kws[0, sl, 124:128] = np.asarray(r["knew"], f).reshape(16, 4, 2, 64)
        vws[0, sl, 0:124] = np.asarray(r["vc_s"], f).reshape(16, 124, 2, 64)
        vws[0, sl, 124:128] = np.asarray(r["vnew"], f).reshape(16, 4, 2, 64)
        hn = np.asarray(r["hs_new"], f)
        srs[0, sl] = hn[0:64].transpose(2, 1, 0); sis[0, sl] = hn[64:128].transpose(2, 1, 0)
    return (y_prompt, y_sample, kwp, vwp, srp, sip, kws, vws, srs, sis)
```
